# Optimizing a Trainium2 kernel written in Bass

```python
import jax, jax.numpy as jnp
from jax import lax
import numpy as np

D_MODEL = 1024
BATCH = 2
SEQ = 8192
DEPTH = 4

CHUNK = 64
EPS = 1e-6
N_BRANCH = 2
CONV_DIM = D_MODEL
CONV_WIDTH = 3
SSM_EXPAND = 2
D_SSM = SSM_EXPAND * D_MODEL
SSM_HEAD_DIM = 64
SSM_HEADS = D_SSM // SSM_HEAD_DIM
SSM_GROUPS = 8
SSM_STATE = 128
SSM_CONV_WIDTH = 4
SSM_CONV_DIM = D_SSM + 2 * SSM_GROUPS * SSM_STATE
DT_MIN = 1e-3
DT_MAX = 1e-1
D_FF = 4 * D_MODEL
N_MOD = 6

kernel_name = "hybrid_shortconv_ssd_gated_trunk"


def proj_sizes():
    return (N_BRANCH * D_MODEL,
            CONV_DIM, CONV_DIM, CONV_DIM,
            D_SSM,
            SSM_CONV_DIM,
            SSM_HEADS)


def rms_norm(x, w):
    xf = x.astype(jnp.float32)
    y = xf * lax.rsqrt(jnp.mean(xf * xf, axis=-1, keepdims=True) + EPS)
    return (y * w.astype(jnp.float32)).astype(x.dtype)


def causal_depthwise_conv(x, w):
    k = w.shape[0]
    return lax.conv_general_dilated(
        x, w[:, None, :].astype(x.dtype), window_strides=(1,), padding=((k - 1, 0),),
        dimension_numbers=('NWC', 'WIO', 'NWC'), feature_group_count=x.shape[-1])


def ssd_scan(x, a, b, c):
    bsz, seqlen, h, p = x.shape
    g, n = b.shape[-2:]
    r = h // g
    nc = seqlen // CHUNK

    def to_chunks(t):
        return jnp.moveaxis(t.reshape(bsz, nc, CHUNK, *t.shape[2:]), 1, 0)

    xs = (to_chunks(x.reshape(bsz, seqlen, g, r, p)),
          to_chunks(a.reshape(bsz, seqlen, g, r)),
          to_chunks(b), to_chunks(c))
    tril = jnp.tril(jnp.ones((CHUNK, CHUNK), dtype=bool))[None, :, :, None, None]

    def step(state, inp):
        xq, aq, bq, cq = inp
        a_cum = jnp.cumsum(aq, axis=1)
        seg = a_cum[:, :, None] - a_cum[:, None, :]
        decay = jnp.exp(jnp.where(tril, seg, -jnp.inf))
        scores = jnp.einsum('btgn,bsgn->btsg', cq, bq)
        y_diag = jnp.einsum('btsg,btsgr,bsgrp->btgrp', scores, decay, xq)
        y_off = jnp.einsum('btgn,bgrpn->btgrp', cq, state) * jnp.exp(a_cum)[..., None]
        to_end = jnp.exp(a_cum[:, -1:] - a_cum)
        new_state = (state * jnp.exp(a_cum[:, -1])[..., None, None]
                     + jnp.einsum('bsgn,bsgr,bsgrp->bgrpn', bq, to_end, xq))
        return new_state, y_diag + y_off

    state0 = jnp.zeros((bsz, g, r, p, n), jnp.float32)
    _, y = lax.scan(step, state0, xs)
    return jnp.moveaxis(y, 0, 1).reshape(bsz, seqlen, h, p)


def mixer_sublayer(u, w_in, conv_w, ssm_conv_w, ssm_conv_b, dt_bias, a_log, d_skip,
                   ssm_norm_w, w_conv_out, w_ssm_out, w_o):
    bsz, seqlen, _ = u.shape
    proj = u @ w_in
    split_at = [int(v) for v in np.cumsum(proj_sizes())[:-1]]
    gl, cb, cc, cx, z, xbc, dt = jnp.split(proj, split_at, axis=-1)

    y_conv = cb * causal_depthwise_conv(cc * cx, conv_w)
    p_conv = y_conv @ w_conv_out

    xbc = jax.nn.silu(causal_depthwise_conv(xbc, ssm_conv_w) + ssm_conv_b.astype(xbc.dtype))
    xbc = xbc.astype(jnp.float32)
    xs, bs, cs = jnp.split(xbc, [D_SSM, D_SSM + SSM_GROUPS * SSM_STATE], axis=-1)
    xs = xs.reshape(bsz, seqlen, SSM_HEADS, SSM_HEAD_DIM)
    bs = bs.reshape(bsz, seqlen, SSM_GROUPS, SSM_STATE)
    cs = cs.reshape(bsz, seqlen, SSM_GROUPS, SSM_STATE)
    dt = jax.nn.softplus(dt.astype(jnp.float32) + dt_bias.astype(jnp.float32))
    a = -jnp.exp(a_log.astype(jnp.float32))
    y = ssd_scan(xs * dt[..., None], dt * a, bs, cs)
    y = y + d_skip.astype(jnp.float32)[:, None] * xs
    y = y.reshape(bsz, seqlen, D_SSM) * jax.nn.silu(z.astype(jnp.float32))
    yg = y.reshape(bsz, seqlen, SSM_GROUPS, D_SSM // SSM_GROUPS)
    yg = yg * lax.rsqrt(jnp.mean(yg * yg, axis=-1, keepdims=True) + EPS)
    y = (yg.reshape(bsz, seqlen, D_SSM) * ssm_norm_w.astype(jnp.float32)).astype(u.dtype)
    p_ssm = y @ w_ssm_out

    g_conv, g_ssm = jnp.split(jax.nn.sigmoid(gl), 2, axis=-1)
    merged = g_conv * p_conv + g_ssm * p_ssm
    return merged @ w_o


def setup_inputs(seed: int = 0) -> dict:
    key = jax.random.key(seed)
    ks = jax.random.split(key, 24)
    d_proj = sum(proj_sizes())
    f32 = jnp.float32

    def nrm(k, shape, scale):
        return jax.random.normal(k, shape, f32) * scale

    dt0 = jnp.exp(jax.random.uniform(ks[10], (DEPTH, SSM_HEADS), f32,
                                     np.log(DT_MIN), np.log(DT_MAX)))
    dt_bias = dt0 + jnp.log(-jnp.expm1(-dt0))
    return {
        "x": nrm(ks[0], (BATCH, SEQ, D_MODEL), 1.0),
        "c": nrm(ks[1], (BATCH, D_MODEL), 1.0),
        "w_ada": nrm(ks[2], (DEPTH, D_MODEL, N_MOD * D_MODEL), 0.5 * D_MODEL ** -0.5),
        "b_ada": nrm(ks[3], (DEPTH, N_MOD * D_MODEL), 0.02),
        "ln1": 1.0 + nrm(ks[4], (DEPTH, D_MODEL), 0.1),
        "ln2": 1.0 + nrm(ks[5], (DEPTH, D_MODEL), 0.1),
        "w_in": nrm(ks[6], (DEPTH, D_MODEL, d_proj), D_MODEL ** -0.5),
        "conv_w": nrm(ks[7], (DEPTH, CONV_WIDTH, CONV_DIM), CONV_WIDTH ** -0.5),
        "ssm_conv_w": nrm(ks[8], (DEPTH, SSM_CONV_WIDTH, SSM_CONV_DIM), SSM_CONV_WIDTH ** -0.5),
        "ssm_conv_b": nrm(ks[9], (DEPTH, SSM_CONV_DIM), 0.01),
        "dt_bias": dt_bias,
        "a_log": jnp.log(jax.random.uniform(ks[11], (DEPTH, SSM_HEADS), f32, 1.0, 16.0)),
        "d_skip": 1.0 + nrm(ks[12], (DEPTH, SSM_HEADS), 0.1),
        "ssm_norm_w": 1.0 + nrm(ks[13], (DEPTH, D_SSM), 0.1),
        "w_conv_out": nrm(ks[14], (DEPTH, CONV_DIM, D_MODEL), CONV_DIM ** -0.5),
        "w_ssm_out": nrm(ks[15], (DEPTH, D_SSM, D_MODEL), D_SSM ** -0.5),
        "w_o": nrm(ks[16], (DEPTH, D_MODEL, D_MODEL), D_MODEL ** -0.5),
        "w_up": nrm(ks[17], (DEPTH, D_MODEL, D_FF), D_MODEL ** -0.5),
        "w_down": nrm(ks[18], (DEPTH, D_FF, D_MODEL), D_FF ** -0.5),
        "final_norm": 1.0 + nrm(ks[19], (D_MODEL,), 0.1),
    }


def reference(x, c, w_ada, b_ada, ln1, ln2, w_in, conv_w, ssm_conv_w, ssm_conv_b,
              dt_bias, a_log, d_skip, ssm_norm_w, w_conv_out, w_ssm_out, w_o,
              w_up, w_down, final_norm):
    bsz = x.shape[0]
    c_act = jax.nn.silu(c)
    for i in range(DEPTH):
        mod = (c_act @ w_ada[i] + b_ada[i]).reshape(bsz, N_MOD, D_MODEL)[:, :, None, :]
        shift1, scale1, gate1 = mod[:, 0], mod[:, 1], mod[:, 2]
        shift2, scale2, gate2 = mod[:, 3], mod[:, 4], mod[:, 5]

        u = rms_norm(x, ln1[i]) * (1.0 + scale1) + shift1
        mix = mixer_sublayer(u, w_in[i], conv_w[i], ssm_conv_w[i], ssm_conv_b[i], dt_bias[i],
                             a_log[i], d_skip[i], ssm_norm_w[i], w_conv_out[i],
                             w_ssm_out[i], w_o[i])
        x = x + gate1 * mix

        u2 = rms_norm(x, ln2[i]) * (1.0 + scale2) + shift2
        hid = jnp.square(jax.nn.relu(u2 @ w_up[i]))
        x = x + gate2 * (hid @ w_down[i])
    return rms_norm(x, final_norm)
```

```python
import contextlib
import numpy as np
import concourse.bass as bass
import concourse.mybir as mybir
from concourse.bass_utils import run_bass_kernel_spmd

F32 = mybir.dt.float32
BF16 = mybir.dt.bfloat16
AF = mybir.ActivationFunctionType
ALU = mybir.AluOpType
AX = mybir.AxisListType

NCORES = 2
D = 1024
SEQ = 8192
NTOK = 8192
T = 512
NT = NTOK // T
DEPTH = 4
DPROJ = 11296
OFF_GL, OFF_CB, OFF_CC, OFF_CX, OFF_Z, OFF_XBC, OFF_DT = 0, 2048, 3072, 4096, 5120, 7168, 11264
EPS = 1e-6
NRING = 3

_cols = {}
_off = 0
for _n, _w in [("c", 8), ("b_ada", 192), ("ln1", 32), ("ln2", 32), ("fn", 8), ("convw", 96),
               ("sconvw", 512), ("sconvb", 128), ("dtb", 128), ("alog", 128), ("dskip", 128),
               ("nw", 64), ("sel", 8), ("mk", 8), ("bt", 64), ("one", 1), ("eps", 1)]:
    _cols[_n] = (_off, _w)
    _off += _w
NPAR = _off


class Res:
    __slots__ = ("name", "w", "r")

    def __init__(self, name):
        self.name = name
        self.w = None
        self.r = {}


class Sem:
    def __init__(self, handle, name):
        self.h = handle
        self.name = name
        self.count = 0


class Tile:
    def __init__(self, t, name):
        self.t = t
        self.res = Res(name)

    def __getitem__(self, idx):
        return self.t[idx]


class View:
    def __init__(self, tile, ap):
        self.t = ap
        self.res = tile.res

    def __getitem__(self, idx):
        return self.t[idx]


class _Stop(Exception):
    pass


class Queue:
    def __init__(self, name, sem, is_pe=False):
        self.name = name
        self.sem = sem
        self.known = {}
        self.prog = []
        self.is_pe = is_pe


def _res(x):
    return x.res if isinstance(x, (Tile, View)) else x


class Builder:
    def __init__(self, nl, debug=(), lim=10 ** 9):
        self.nl = nl
        self.lim = lim
        self.debug = tuple(debug)
        self.nc = bass.Bass("TRN2", target_bir_lowering=False)
        self.stack = contextlib.ExitStack()
        self.dbg_out = {}
        self.n_cc = 0

    def sem(self, name):
        return Sem(self.stack.enter_context(self.nc.semaphore(name)), name)

    def sb(self, name, shape, dt):
        return Tile(self.stack.enter_context(self.nc.sbuf_tensor("sb_" + name, list(shape), dt)), name)

    def ps(self, name, shape, dt):
        return Tile(self.stack.enter_context(self.nc.psum_tensor("ps_" + name, list(shape), dt)), name)

    def _deps(self, E, reads, writes):
        need = {}

        def add(tok):
            if tok is None:
                return
            s, v = tok
            if need.get(s, 0) < v:
                need[s] = v

        for r in reads:
            add(r.w)
        for r in writes:
            add(r.w)
            for s, v in r.r.items():
                add((s, v))
        for s, v in need.items():
            if E.is_pe and s is E.sem:
                continue
            if E.known.get(s, 0) < v:
                E.known[s] = v
                E.prog.append(("wait", s, v))

    def _mark(self, tok, reads, writes):
        s, v = tok
        for r in reads:
            if r.r.get(s, 0) < v:
                r.r[s] = v
        for r in writes:
            r.w = tok
            r.r = {}

    def op(self, E, fn, reads=(), writes=(), inc=True):
        reads = [_res(x) for x in reads]
        writes = [_res(x) for x in writes]
        self._deps(E, reads, writes)
        if inc:
            E.sem.count += 1
            tok = (E.sem, E.sem.count)
        else:
            tok = (E.sem, E.sem.count + 1)
        E.prog.append(("op", fn, inc))
        self._mark(tok, reads, writes)

    def dma(self, E, fn, reads=(), writes=(), dsem=None):
        reads = [_res(x) for x in reads]
        writes = [_res(x) for x in writes]
        if dsem is None:
            pool = self.dsems[E.name]
            dsem = pool[self.dsem_rr[E.name] % len(pool)]
            self.dsem_rr[E.name] += 1
        if E.known.get(dsem, 0) < dsem.count:
            E.known[dsem] = dsem.count
            E.prog.append(("wait", dsem, dsem.count))
        self._deps(E, reads, writes)
        dsem.count += 16
        tok = (dsem, dsem.count)
        E.prog.append(("dma", fn, dsem))
        self._mark(tok, reads, writes)
        return tok

    def collective(self, cin, cout, r_in, r_out):
        POOL = self.POOL
        for sm in self.wsems + self.dsems["pool"]:
            if POOL.known.get(sm, 0) < sm.count:
                POOL.known[sm] = sm.count
                POOL.prog.append(("wait", sm, sm.count))
        ccs = self.sem("cc%d" % self.n_cc)
        self.n_cc += 1
        reads = [r_in]
        writes = [r_out]
        self._deps(POOL, reads, writes)
        POOL.prog.append(("cc", lambda e: e.collective_compute("AllGather", ALU.bypass,
                                                               replica_groups=[list(range(NCORES))],
                                                               ins=[cin], outs=[cout]), ccs))
        self._mark((ccs, 1), reads, writes)
        POOL.known[ccs] = 1
        POOL.prog.append(("wait", ccs, 1))

    def chk(self, n):
        if n >= self.lim:
            raise _Stop()

    def mm(self, out, lhsT, rhs, start, stop, reads, writes):
        self.op(self.PE, lambda e: e.matmul(out, lhsT=lhsT, rhs=rhs, start=start, stop=stop),
                reads, writes, inc=stop)

    def tr(self, out, in_, ident, reads, writes):
        self.op(self.PE, lambda e: e.transpose(out, in_, ident), list(reads) + [self.consts], writes)

    def act(self, out, in_, func, reads, writes, bias=None, scale=None):
        kw = {}
        if bias is not None:
            kw["bias"] = bias
        if scale is not None:
            kw["scale"] = scale
        self.op(self.ACT, lambda e: e.activation(out=out, in_=in_, func=func, **kw), reads, writes)

    def tt(self, E, out, in0, in1, op, reads, writes):
        self.op(E, lambda e: e.tensor_tensor(out=out, in0=in0, in1=in1, op=op), reads, writes)

    def ts(self, E, out, in0, s1, op0, reads, writes, s2=None, op1=None):
        if op1 is None:
            self.op(E, lambda e: e.tensor_scalar(out=out, in0=in0, scalar1=s1, scalar2=None, op0=op0),
                    reads, writes)
        else:
            self.op(E, lambda e: e.tensor_scalar(out=out, in0=in0, scalar1=s1, scalar2=s2, op0=op0, op1=op1),
                    reads, writes)

    def stt(self, out, in0, scalar, in1, op0, op1, reads, writes):
        self.op(self.DVE, lambda e: e.scalar_tensor_tensor(out=out, in0=in0, scalar=scalar, in1=in1,
                                                           op0=op0, op1=op1), reads, writes)

    def cp(self, E, out, in_, reads, writes):
        if E is self.ACT:
            self.op(E, lambda e: e.activation(out=out, in_=in_, func=AF.Copy), reads, writes)
        else:
            self.op(E, lambda e: e.tensor_copy(out=out, in_=in_), reads, writes)

    def pbank(self):
        b = self.pbanks[self.pb_rr % len(self.pbanks)]
        self.pb_rr += 1
        return b

    def tbank(self):
        b = self.tbanks[self.tb_rr % len(self.tbanks)]
        self.tb_rr += 1
        return b

    def par(self, name, lo=0, hi=None):
        o, w = _cols[name]
        hi = w if hi is None else hi
        return self.params[:, o + lo:o + hi]

    def dbg(self, name, tile, ap, shape, dt=F32):
        if name not in self.debug:
            return
        d = self.nc.dram_tensor("dbg_" + name, list(shape), dt, kind="ExternalOutput").ap()
        self.dbg_out[name] = d
        self.dma(self.SP, lambda e: e.dma_start(out=d, in_=ap), [tile], [self.dbg_res], dsem=self.out_sem)

    def wnext(self):
        while self.w_issued < len(self.wplan) and self.w_issued < self.w_used + NRING:
            pieces = self.wplan[self.w_issued]
            slot = self.ring[self.w_issued % NRING]
            dsem = self.wsems[self.w_issued % NRING]
            for (src, dst_fn) in pieces:
                dst = dst_fn(slot)
                self.dma(self.POOL, (lambda s, d: (lambda e: e.dma_start(out=d, in_=s)))(src, dst),
                         [], [slot], dsem=dsem)
            self.w_issued += 1
        slot = self.ring[self.w_used % NRING]
        self.w_used += 1
        return slot

    @staticmethod
    def wview(slot, kc, w):
        return slot.t[:, 0:kc * w].rearrange("p (j c) -> p j c", j=kc)

    def plan_weights(self, W):
        plan = []

        def blk(mat, c0, w, kc=8, r0=0):
            src = mat[r0:r0 + kc * 128, c0:c0 + w].rearrange("(j p) c -> p j c", p=128)
            plan.append([(src, lambda s, kc=kc, w=w: Builder.wview(s, kc, w))])

        for l in range(self.nl):
            for b in range(12):
                blk(W["w_ada"][l], b * 512, 512)
        for l in range(self.nl):
            win = W["w_in"][l]
            for it in range(NT):
                for b in range(4):
                    blk(win, OFF_GL + b * 512, 512)
                for off in (OFF_CX, OFF_CC, OFF_CB):
                    for b in range(2):
                        blk(win, off + b * 512, 512)
                for b in range(2):
                    blk(W["w_conv_out"][l], b * 512, 512)
                for b in range(4):
                    blk(win, OFF_Z + b * 512, 512)
                blk(win, OFF_DT + 32 - 512, 512)
                for g in range(8):
                    pcs = []
                    for (c0, w, d0) in ((OFF_XBC + 256 * g, 256, 0), (OFF_XBC + 2048 + 128 * g, 128, 256),
                                        (OFF_XBC + 3072 + 128 * g, 128, 384)):
                        src = win[:, c0:c0 + w].rearrange("(j p) c -> p j c", p=128)
                        pcs.append((src, lambda s, d0=d0, w=w: Builder.wview(s, 8, 512)[:, :, d0:d0 + w]))
                    plan.append(pcs)
            for it in range(NT):
                for b in range(4):
                    blk(W["w_ssm_out"][l], b * 256, 256, kc=16)
                for b in range(2):
                    blk(W["w_o"][l], b * 512, 512)
                for fg in range(4):
                    for b in range(2):
                        blk(W["w_up"][l], fg * 1024 + b * 512, 512)
                    for b in range(2):
                        blk(W["w_down"][l], b * 512, 512, r0=fg * 1024)
        self.wplan = plan
        self.w_issued = 0
        self.w_used = 0

    def build(self):
        nc = self.nc
        nl = self.nl
        ins = {}

        def din(name, shape, dt=F32):
            t = nc.dram_tensor(name, list(shape), dt, kind="ExternalInput").ap()
            ins[name] = t
            return t

        x_in = din("x", [NTOK, D])
        params_in = din("params", [128, NPAR])
        consts_in = din("consts", [128, 512])
        W = {}
        W["w_ada"] = din("w_ada", [max(nl, 1), D, 6 * D])
        W["w_in"] = din("w_in", [max(nl, 1), D, DPROJ])
        W["w_conv_out"] = din("w_conv_out", [max(nl, 1), D, D])
        W["w_ssm_out"] = din("w_ssm_out", [max(nl, 1), 2 * D, D])
        W["w_o"] = din("w_o", [max(nl, 1), D, D])
        W["w_up"] = din("w_up", [max(nl, 1), D, 4 * D])
        W["w_down"] = din("w_down", [max(nl, 1), 4 * D, D])
        out = nc.dram_tensor("out", [NTOK, D], F32, kind="ExternalOutput").ap()

        sp_gssm = nc.dram_tensor("sp_gssm", [NT, 128, 8 * T], BF16).ap()
        sp_gcpc = nc.dram_tensor("sp_gcpc", [NT, 128, 8 * T], BF16).ap()
        sp_sz = nc.dram_tensor("sp_sz", [NT, 128, 4 * 2048], BF16).ap()
        sp_yl = nc.dram_tensor("sp_yl", [NT, 128, 4 * 2048], BF16).ap()
        sp_ct = nc.dram_tensor("sp_ct", [NT, 128, 8 * T], BF16).ap()
        cc_st_in = nc.dram_tensor("cc_st_in", [128, 2048 + 32], F32).ap()
        cc_st_out = nc.dram_tensor("cc_st_out", [NCORES * 128, 2048 + 32], F32).ap()
        cc_h_in = nc.dram_tensor("cc_h_in", [128, 112], F32).ap()
        cc_h_out = nc.dram_tensor("cc_h_out", [NCORES * 128, 112], F32).ap()
        xres = nc.dram_tensor("xres", [128, 8, NTOK], F32).ap()
        R_x = [Res("xres%d" % i) for i in range(NT)]
        R_gssm = [Res("sp_gssm%d" % i) for i in range(NT)]
        R_gcpc = [Res("sp_gcpc%d" % i) for i in range(NT)]
        R_sz = [Res("sp_sz%d" % i) for i in range(NT)]
        R_yl = [Res("sp_yl%d" % i) for i in range(NT)]
        R_ct = [Res("sp_ct%d" % i) for i in range(NT)]
        R_ccst_in, R_ccst_out = Res("ccsti"), Res("ccsto")
        R_cch_in, R_cch_out = Res("cchi"), Res("ccho")
        self.dbg_res = Res("dbg")

        self.PE = Queue("pe", self.sem("s_pe"), is_pe=True)
        self.ACT = Queue("act", self.sem("s_act"))
        self.DVE = Queue("dve", self.sem("s_dve"))
        self.POOL = Queue("pool", self.sem("s_pool"))
        self.SP = Queue("sp", self.sem("s_sp"))
        PE, ACT, DVE, POOL, SP = self.PE, self.ACT, self.DVE, self.POOL, self.SP
        self.dsems = {"sp": [self.sem("d_sp%d" % i) for i in range(16)],
                      "pool": [self.sem("d_pl%d" % i) for i in range(4)]}
        self.dsem_rr = {"sp": 0, "pool": 0}
        self.wsems = [self.sem("d_w%d" % i) for i in range(NRING)]
        self.out_sem = self.sem("d_out")

        xT = self.sb("xT", [128, 8, T], F32)
        self.ring = [self.sb("ring%d" % i, [128, 4096], BF16) for i in range(NRING)]
        self.params = None
        params = self.sb("params", [128, NPAR], F32)
        self.params = params.t
        self.consts = consts = self.sb("consts", [128, 512], F32)
        cbf = self.sb("cbf", [128, 640], BF16)
        modT = self.sb("modT", [128, 192], F32)
        lay = self.sb("lay", [128, 6, 8], F32)
        arep = self.sb("arep", [128, 32], F32)
        cact = self.sb("cact", [128, 8], BF16)
        state = self.sb("state", [128, 8, 256], F32)
        state_bf = self.sb("state_bf", [128, 8, 256], BF16)
        sstart = state_bf
        ccx_prev = self.sb("ccx_prev", [128, 8, 2], BF16)
        xbc_prev = self.sb("xbc_prev", [128, 32, 3], BF16)
        eseg = self.sb("eseg", [128, 4, 32], F32)
        arun = self.sb("arun", [128, 32], F32)
        f32a = self.sb("f32a", [128, 8, 256], F32)
        uT = self.sb("uT", [128, 8, T], BF16)
        rstd = self.sb("rstd", [128, T], F32)
        gA = self.sb("gA", [128, 8, T], BF16)
        gB = self.sb("gB", [128, 8, T], BF16)
        p3 = self.sb("p3", [128, 8, T], BF16)
        p4 = self.sb("p4", [128, 8, T + 2], BF16)
        p5 = self.sb("p5", [128, 8, T], BF16)
        zb = [self.sb("zb%d" % i, [128, 512], BF16) for i in range(2)]
        raw = [self.sb("raw%d" % i, [128, 4, T + 3], BF16) for i in range(2)]
        accf = [self.sb("accf%d" % i, [128, T], F32) for i in range(2)]
        xTg = [self.sb("xTg%d" % i, [128, 2, T], BF16) for i in range(1)]
        bTg = [self.sb("bTg%d" % i, [128, T], BF16) for i in range(2)]
        xtm = [self.sb("xtm%d" % i, [128, 4, 256], BF16) for i in range(1)]
        btm = [self.sb("btm%d" % i, [128, 4, 128], BF16) for i in range(1)]
        dtt = self.sb("dtt", [128, 4, 32], F32)
        att = self.sb("att", [128, 4, 32], F32)
        a3 = [self.sb("a3_%d" % i, [128, 4, 32], BF16) for i in range(3)]
        acum = self.sb("acum", [128, 4, 32], F32)
        eloc = self.sb("eloc", [128, 4, 32], F32)
        toend = self.sb("toend", [128, 4, 32], F32)
        dk = self.sb("dk", [128, 4, 32], F32)
        tmp32 = self.sb("tmp32", [128, 4, 32], F32)
        arhs = [self.sb("arhs%d" % i, [128, 4, 128], BF16) for i in range(2)]
        eseg_g = [self.sb("esg%d" % i, [128, 4, 128], BF16) for i in range(2)]
        msc = [self.sb("msc%d" % i, [128, 128], BF16) for i in range(2)]
        mT = [self.sb("mT%d" % i, [128, 4, 128], BF16) for i in range(2)]
        xdt = [self.sb("xdt%d" % i, [128, 256], BF16) for i in range(2)]
        xdte = [self.sb("xdte%d" % i, [128, 256], BF16) for i in range(2)]
        t1 = [self.sb("t1_%d" % i, [128, 256], F32) for i in range(1)]
        t2 = [self.sb("t2_%d" % i, [128, 256], F32) for i in range(1)]
        hsel = View(t2[0], t2[0].t[:, 0:112])
        atall = View(t1[0], t1[0].t[:, :].rearrange("p (a b) -> p a b", a=8))
        coef = View(t2[0], t2[0].t[:, :].rearrange("p (a b) -> p a b", a=8))
        u3 = self.sb("u3", [128, 8, 4], BF16)
        sm3 = self.sb("sm3", [128, 8, 4], F32)
        r3 = self.sb("r3", [128, 4], F32)
        hraw = View(t1[0], t1[0].t[:, 0:144].rearrange("p (a b) -> p a b", b=3))
        ssq = self.sb("ssq", [128, 8], F32)
        rs8 = self.sb("rs8", [128, 8], F32)

        self.pbanks = [self.ps("pb%d" % i, [128, 512], F32) for i in range(6)]
        self.tbanks = [self.ps("tbk%d" % i, [128, 1024], BF16) for i in range(2)]
        self.pb_rr = 0
        self.tb_rr = 0

        ident_f = consts.t[:, 0:128]
        triu_f = consts.t[:, 128:256]
        lmat_f = consts.t[:, 256:384]
        ones_f = consts.t[:, 384:512]
        ident_b = cbf.t[:, 0:128]
        onesd_b = cbf.t[:, 128:256]
        triu_b = cbf.t[:, 256:384]
        lmat_b = cbf.t[:, 384:512]
        ones_b = cbf.t[:, 512:640]

        self.plan_weights(W)
        HSMV = f32a.t[:, :, :].rearrange("p a b -> p (a b)")[:, 0:896].rearrange("p (a b) -> p a b", a=8)

        self.dma(SP, lambda e: e.dma_start(out=params.t[:], in_=params_in), [], [params])
        self.dma(SP, lambda e: e.dma_start(out=consts.t[:], in_=consts_in), [], [consts])
        self.cp(DVE, ident_b, ident_f, [consts], [cbf])
        self.ts(DVE, onesd_b, ones_f, 1.0 / 1024.0, ALU.mult, [consts, cbf], [cbf])
        self.cp(DVE, cbf.t[:, 256:640], consts.t[:, 128:512], [consts, cbf], [cbf])
        for it in range(NT):
            for t4 in range(4):
                tb = it * 4 + t4
                for h in range(2):
                    st = accf[(tb * 2 + h) % 2]
                    self.dma(SP, (lambda tb, h, st: (lambda e: e.dma_start(
                        out=st.t[:, :], in_=x_in[tb * 128:(tb + 1) * 128, h * 512:(h + 1) * 512])))(tb, h, st), [], [st])
                    pb = self.pbank()
                    for jj in range(4):
                        self.tr(pb.t[:, jj * 128:(jj + 1) * 128], st.t[:, jj * 128:(jj + 1) * 128], ident_f, [st], [pb])
                    self.cp(ACT if h else DVE, xT.t[:, h * 4:(h + 1) * 4, t4 * 128:(t4 + 1) * 128],
                            pb.t[:].rearrange("p (a b) -> p a b", a=4), [pb], [xT])
            self.dma(SP, (lambda it: (lambda e: e.dma_start(out=xres[:, :, it * T:(it + 1) * T], in_=xT.t[:, :, :])))(it),
                     [xT], [R_x[it]])
        self.act(cact.t[:], self.par("c"), AF.Silu, [params], [cact])
        if nl > 0:
            pmod = self.pbank()
            for l in range(nl):
                for b in range(12):
                    slot = self.wnext()
                    wv = self.wview(slot, 8, 512)
                    for cb_ in range(4):
                        col = l * 48 + b * 4 + cb_
                        for k in range(8):
                            self.mm(pmod.t[:, col:col + 1], wv[:, k, cb_ * 128:(cb_ + 1) * 128], cact.t[:, k:k + 1],
                                    k == 0, k == 7, [slot, cact], [pmod])
            self.tt(DVE, modT.t[:, 0:48 * nl], pmod.t[:, 0:48 * nl], self.par("b_ada", 0, 48 * nl), ALU.add,
                    [pmod, params], [modT])

        def norm_to_u(ts_, gcol, shcol):
            for h in range(2):
                tsl = slice(h * 256, (h + 1) * 256)
                self.act(p3.t[:, :, h * 256:(h + 1) * 256], xT.t[:, :, tsl], AF.Square, [xT], [p3])
            pb = self.pbank()
            for k in range(8):
                self.mm(pb.t[:, :], onesd_b, p3.t[:, k, :], k == 0, k == 7, [cbf, p3], [pb])
            self.act(rstd.t[:, :], pb.t[:, :], AF.Ln, [pb, params], [rstd], bias=self.par("eps"))
            self.act(rstd.t[:, :], rstd.t[:, :], AF.Exp, [rstd], [rstd], scale=-0.5)
            for h in range(2):
                tsl = slice(h * 256, (h + 1) * 256)
                self.tt(DVE, f32a.t[:, :, :], xT.t[:, :, tsl],
                        rstd.t[:, h * 256:(h + 1) * 256].unsqueeze(1).to_broadcast([128, 8, 256]),
                        ALU.mult, [xT, rstd], [f32a])
                for j in range(8):
                    self.act(uT.t[:, j, h * 256:(h + 1) * 256], f32a.t[:, j, :], AF.Identity, [f32a, lay], [uT],
                             bias=lay.t[:, shcol, j:j + 1], scale=lay.t[:, gcol, j:j + 1])

        try:
            self.layers(locals())
        except _Stop:
            for i in range(NRING):
                if POOL.known.get(self.wsems[i], 0) < self.wsems[i].count:
                    POOL.prog.append(("wait", self.wsems[i], self.wsems[i].count))
        return self.finish(locals())

    def layers(self, env):
        (nl, xT, params, consts, cbf, modT, lay, arep, cact, state, state_bf, sstart, ccx_prev, xbc_prev, eseg, arun,
         f32a, uT, rstd, gA, gB, p3, p4, p5, zb, raw, accf, xTg, bTg, xtm, btm, dtt, att, a3, acum, eloc, toend, dk, tmp32,
         arhs, eseg_g, msc, mT, xdt, xdte, t1, t2, hsel, atall, coef, u3, sm3, r3, hraw, ssq, rs8,
         ident_f, triu_f, lmat_f, ones_f, ident_b, onesd_b, triu_b, lmat_b, ones_b, HSMV, norm_to_u,
         sp_gssm, sp_gcpc, sp_sz, sp_yl, sp_ct, cc_st_in, cc_st_out, cc_h_in, cc_h_out,
         R_gssm, R_gcpc, R_sz, R_yl, R_ct, R_ccst_in, R_ccst_out, R_cch_in, R_cch_out,
         PE, ACT, DVE, POOL, SP, xres, R_x) = [env[k] for k in (
            "nl xT params consts cbf modT lay arep cact state state_bf sstart ccx_prev xbc_prev eseg arun "
            "f32a uT rstd gA gB p3 p4 p5 zb raw accf xTg bTg xtm btm dtt att a3 acum eloc toend dk tmp32 "
            "arhs eseg_g msc mT xdt xdte t1 t2 hsel atall coef u3 sm3 r3 hraw ssq rs8 "
            "ident_f triu_f lmat_f ones_f ident_b onesd_b triu_b lmat_b ones_b HSMV norm_to_u "
            "sp_gssm sp_gcpc sp_sz sp_yl sp_ct cc_st_in cc_st_out cc_h_in cc_h_out "
            "R_gssm R_gcpc R_sz R_yl R_ct R_ccst_in R_ccst_out R_cch_in R_cch_out "
            "PE ACT DVE POOL SP xres R_x").split()]
        for l in range(nl):
            mo = l * 48
            self.stt(lay.t[:, 0, :], modT.t[:, mo + 8:mo + 16], 1.0, self.par("ln1", l * 8, l * 8 + 8),
                     ALU.add, ALU.mult, [modT, params], [lay])
            self.cp(DVE, lay.t[:, 1, :], modT.t[:, mo:mo + 8], [modT], [lay])
            self.cp(DVE, lay.t[:, 2, :], modT.t[:, mo + 16:mo + 24], [modT], [lay])
            self.stt(lay.t[:, 3, :], modT.t[:, mo + 32:mo + 40], 1.0, self.par("ln2", l * 8, l * 8 + 8),
                     ALU.add, ALU.mult, [modT, params], [lay])
            self.cp(DVE, lay.t[:, 4, :], modT.t[:, mo + 24:mo + 32], [modT], [lay])
            self.cp(DVE, lay.t[:, 5, :], modT.t[:, mo + 40:mo + 48], [modT], [lay])
            self.act(arep.t[:, :], self.par("alog", l * 32, l * 32 + 32), AF.Exp, [params], [arep])
            self.ts(DVE, arep.t[:, :], arep.t[:, :], -1.0, ALU.mult, [arep], [arep])
            self.op(POOL, lambda e: e.memset(state.t[:], 0.0), [], [state])
            self.op(POOL, lambda e: e.memset(state_bf.t[:], 0.0), [], [state_bf])
            self.op(POOL, lambda e: e.memset(arun.t[:], 0.0), [], [arun])
            self.chk(1)

            cw = lambda k, j: self.par("convw", l * 24 + k * 8 + j, l * 24 + k * 8 + j + 1)
            sw = lambda k, i: self.par("sconvw", l * 128 + k * 32 + i, l * 128 + k * 32 + i + 1)
            sbias = lambda i: self.par("sconvb", l * 32 + i, l * 32 + i + 1)

            self.op(POOL, lambda e: e.memset(xbc_prev.t[:], 0.0), [], [xbc_prev])
            self.op(POOL, lambda e: e.memset(ccx_prev.t[:], 0.0), [], [ccx_prev])
            self.chk(3)

            for it in range(NT):
                ts_ = it * T
                self.dma(SP, (lambda it: (lambda e: e.dma_start(out=xT.t[:, :, :], in_=xres[:, :, it * T:(it + 1) * T])))(it),
                         [R_x[it]], [xT])
                norm_to_u(ts_, 0, 1)
                self.chk(4)
                for b in range(4):
                    slot = self.wnext()
                    wv = self.wview(slot, 8, 512)
                    for c4 in range(4):
                        gb = b * 4 + c4
                        pb = self.pbank()
                        for k in range(8):
                            self.mm(pb.t[:, :], wv[:, k, c4 * 128:(c4 + 1) * 128], uT.t[:, k, :], k == 0, k == 7,
                                    [slot, uT], [pb])
                        dst = gA if gb < 8 else gB
                        self.act(dst.t[:, gb % 8, :], pb.t[:, :], AF.Sigmoid, [pb], [dst])
                self.dma(SP, (lambda it: (lambda e: e.dma_start(out=sp_gssm[it].rearrange("p (a b) -> p a b", a=8),
                                                                 in_=gB.t[:, :, :])))(it), [gB], [R_gssm[it]])
                self.chk(5)
                self.cp(POOL, p4.t[:, :, 0:2], ccx_prev.t[:, :, :], [ccx_prev], [p4])
                for fam in range(3):
                    for b in range(2):
                        slot = self.wnext()
                        wv = self.wview(slot, 8, 512)
                        for c4 in range(4):
                            j = b * 4 + c4
                            pb = self.pbank()
                            for k in range(8):
                                self.mm(pb.t[:, :], wv[:, k, c4 * 128:(c4 + 1) * 128], uT.t[:, k, :], k == 0, k == 7,
                                        [slot, uT], [pb])
                            if fam == 0:
                                self.act(p3.t[:, j, :], pb.t[:, :], AF.Copy, [pb], [p3])
                            elif fam == 1:
                                self.tt(DVE, p4.t[:, j, 2:T + 2], pb.t[:, :], p3.t[:, j, :], ALU.mult, [pb, p3], [p4])
                            else:
                                self.act(p3.t[:, j, :], pb.t[:, :], AF.Copy, [pb], [p3])
                self.cp(POOL, ccx_prev.t[:, :, :], p4.t[:, :, T:T + 2], [p4], [ccx_prev])
                for j in range(8):
                    ac = accf[j % 2]
                    self.ts(DVE, ac.t[:, :], p4.t[:, j, 0:T], cw(0, j), ALU.mult, [p4, params], [ac])
                    self.stt(ac.t[:, :], p4.t[:, j, 1:T + 1], cw(1, j), ac.t[:, :], ALU.mult, ALU.add,
                             [p4, params, ac], [ac])
                    self.stt(ac.t[:, :], p4.t[:, j, 2:T + 2], cw(2, j), ac.t[:, :], ALU.mult, ALU.add,
                             [p4, params, ac], [ac])
                    self.tt(DVE, p5.t[:, j, :], ac.t[:, :], p3.t[:, j, :], ALU.mult, [ac, p3], [p5])
                for b in range(2):
                    slot = self.wnext()
                    wv = self.wview(slot, 8, 512)
                    for c4 in range(4):
                        ob = b * 4 + c4
                        pb = self.pbank()
                        for k in range(8):
                            self.mm(pb.t[:, :], wv[:, k, c4 * 128:(c4 + 1) * 128], p5.t[:, k, :], k == 0, k == 7,
                                    [slot, p5], [pb])
                        self.tt(DVE, gB.t[:, ob, :], pb.t[:, :], gA.t[:, ob, :], ALU.mult, [pb, gA], [gB])
                self.dma(SP, (lambda it: (lambda e: e.dma_start(out=sp_gcpc[it].rearrange("p (a b) -> p a b", a=8),
                                                                 in_=gB.t[:, :, :])))(it), [gB], [R_gcpc[it]])
                self.chk(6)
                for zc in range(4):
                    slot = self.wnext()
                    wv = self.wview(slot, 8, 512)
                    for tb in range(4):
                        pb = self.pbank()
                        for k in range(8):
                            self.mm(pb.t[:, :], uT.t[:, k, tb * 128:(tb + 1) * 128], wv[:, k, :], k == 0, k == 7,
                                    [slot, uT], [pb])
                        zt = zb[(zc * 4 + tb) % 2]
                        self.act(zt.t[:, 0:512], pb.t[:, :], AF.Silu, [pb], [zt])
                        self.dma(SP, (lambda it, tb, zc, zt: (lambda e: e.dma_start(
                            out=sp_sz[it][:, tb * 2048 + zc * 512: tb * 2048 + (zc + 1) * 512],
                            in_=zt.t[:, 0:512])))(it, tb, zc, zt), [zt], [R_sz[it]])
                self.chk(7)
                slot = self.wnext()
                wv = self.wview(slot, 8, 512)
                pb = self.pbank()
                for tb in range(4):
                    for k in range(8):
                        self.mm(pb.t[:, tb * 32:(tb + 1) * 32], uT.t[:, k, tb * 128:(tb + 1) * 128], wv[:, k, 480:512],
                                k == 0, k == 7, [slot, uT], [pb])
                self.tt(DVE, dtt.t[:, :, :], pb.t[:, 0:128].rearrange("p (a b) -> p a b", a=4),
                        self.par("dtb", l * 32, l * 32 + 32).unsqueeze(1).to_broadcast([128, 4, 32]), ALU.add,
                        [pb, params], [dtt])
                self.act(dtt.t[:, :, :], dtt.t[:, :, :], AF.Exp, [dtt], [dtt])
                self.act(dtt.t[:, :, :], dtt.t[:, :, :], AF.Ln, [dtt, params], [dtt], bias=self.par("one"))
                self.tt(DVE, att.t[:, :, :], dtt.t[:, :, :], arep.t[:, :].unsqueeze(1).to_broadcast([128, 4, 32]),
                        ALU.mult, [dtt, arep], [att])
                self.chk(7.1)
                self.cp(DVE, a3[0].t[:, :, :], att.t[:, :, :], [att], [a3[0]])
                self.tt(DVE, tmp32.t[:, :, :], att.t[:, :, :], a3[0].t[:, :, :], ALU.subtract, [att, a3[0]], [tmp32])
                self.cp(DVE, a3[1].t[:, :, :], tmp32.t[:, :, :], [tmp32], [a3[1]])
                self.tt(DVE, tmp32.t[:, :, :], tmp32.t[:, :, :], a3[1].t[:, :, :], ALU.subtract, [tmp32, a3[1]], [tmp32])
                self.cp(DVE, a3[2].t[:, :, :], tmp32.t[:, :, :], [tmp32], [a3[2]])
                self.chk(7.2)
                for tb in range(4):
                    pb = self.pbank()
                    for xi in range(3):
                        self.mm(pb.t[:, 0:32], triu_b, a3[xi].t[:, tb, :], xi == 0, xi == 2, [cbf, a3[xi]], [pb])
                    self.cp(DVE, acum.t[:, tb, :], pb.t[:, 0:32], [pb], [acum])
                    pb2 = self.pbank()
                    for xi in range(3):
                        self.mm(pb2.t[:, 0:32], ones_b, a3[xi].t[:, tb, :], xi == 0, xi == 2, [cbf, a3[xi]], [pb2])
                    self.cp(DVE, tmp32.t[:, tb, :], pb2.t[:, 0:32], [pb2], [tmp32])
                    self.act(eloc.t[:, tb, :], acum.t[:, tb, :], AF.Exp, [acum], [eloc])
                    self.act(dk.t[:, tb, :], tmp32.t[:, tb, :], AF.Exp, [tmp32], [dk])
                    self.tt(DVE, toend.t[:, tb, :], tmp32.t[:, tb, :], acum.t[:, tb, :], ALU.subtract, [tmp32, acum], [toend])
                    self.act(toend.t[:, tb, :], toend.t[:, tb, :], AF.Exp, [toend], [toend])
                    self.tt(DVE, arun.t[:, :], arun.t[:, :], tmp32.t[:, tb, :], ALU.add, [arun, tmp32], [arun])
                self.chk(8)
                ct_all = gA
                for g in range(8):
                    slot = self.wnext()
                    wv = self.wview(slot, 8, 512)
                    rw = raw[g % 2]
                    idxs = (2 * g, 2 * g + 1, 16 + g, 24 + g)
                    for ob in range(4):
                        self.cp(POOL, rw.t[:, ob, 0:3], xbc_prev.t[:, idxs[ob], :], [xbc_prev], [rw])
                    for ob in range(4):
                        pb = self.pbank()
                        for k in range(8):
                            self.mm(pb.t[:, :], wv[:, k, ob * 128:(ob + 1) * 128], uT.t[:, k, :], k == 0, k == 7,
                                    [slot, uT], [pb])
                        self.act(rw.t[:, ob, 3:T + 3], pb.t[:, :], AF.Copy, [pb], [rw])
                    for ob in range(4):
                        self.cp(POOL, xbc_prev.t[:, idxs[ob], :], rw.t[:, ob, T:T + 3], [rw], [xbc_prev])
                    xg = xTg[0]
                    bg = bTg[g % 2]
                    for ob in range(4):
                        ci = idxs[ob]
                        ac = accf[ob % 2]
                        self.ts(DVE, ac.t[:, :], rw.t[:, ob, 0:T], sw(0, ci), ALU.mult, [rw, params], [ac],
                                s2=sbias(ci), op1=ALU.add)
                        for kk in range(1, 4):
                            self.stt(ac.t[:, :], rw.t[:, ob, kk:T + kk], sw(kk, ci), ac.t[:, :], ALU.mult, ALU.add,
                                     [rw, params, ac], [ac])
                        if ob < 2:
                            self.act(xg.t[:, ob, :], ac.t[:, :], AF.Silu, [ac], [xg])
                        elif ob == 2:
                            self.act(bg.t[:, :], ac.t[:, :], AF.Silu, [ac], [bg])
                        else:
                            self.act(ct_all.t[:, g, :], ac.t[:, :], AF.Silu, [ac], [ct_all])
                    xm = xtm[0]
                    bm = btm[0]
                    tbk = self.tbank()
                    for tb in range(4):
                        for ob in range(2):
                            self.tr(tbk.t[:, (tb * 2 + ob) * 128:(tb * 2 + ob + 1) * 128],
                                    xg.t[:, ob, tb * 128:(tb + 1) * 128], ident_b, [xg, cbf], [tbk])
                    self.cp(ACT, xm.t[:, :, :], tbk.t[:, :].rearrange("p (a b) -> p a b", a=4), [tbk], [xm])
                    tbk = self.tbank()
                    for tb in range(4):
                        self.tr(tbk.t[:, tb * 128:(tb + 1) * 128], bg.t[:, tb * 128:(tb + 1) * 128], ident_b,
                                [bg, cbf], [tbk])
                    self.cp(ACT, bm.t[:, :, :], tbk.t[:, 0:512].rearrange("p (a b) -> p a b", a=4), [tbk], [bm])
                    yl_dst = p4 if True else None
                    for tb in range(4):
                        i2 = (g * 4 + tb) % 2
                        hs = slice(4 * g, 4 * g + 4)
                        pseg = self.pbank()
                        for xi in range(2):
                            ar = arhs[xi]
                            self.tt(POOL, ar.t[:, :, :], a3[xi].t[:, tb, hs].unsqueeze(2).to_broadcast([128, 4, 128]),
                                    triu_b.unsqueeze(1).to_broadcast([128, 4, 128]), ALU.mult, [a3[xi], cbf], [ar])
                            self.mm(pseg.t[:, :], lmat_b, ar.t[:, :, :].rearrange("p a b -> p (a b)"), xi == 0, xi == 1,
                                    [cbf, ar], [pseg])
                        eg = eseg_g[i2]
                        self.act(eg.t[:, :, :], pseg.t[:, :].rearrange("p (a b) -> p a b", a=4), AF.Exp, [pseg], [eg])
                        psc = self.pbank()
                        self.mm(psc.t[:, 0:128], bg.t[:, tb * 128:(tb + 1) * 128], ct_all.t[:, g, tb * 128:(tb + 1) * 128],
                                True, True, [bg, ct_all], [psc])
                        ms = msc[i2]
                        self.tt(DVE, ms.t[:, :], psc.t[:, 0:128], triu_f, ALU.mult, [psc, consts], [ms])
                        mt = mT[i2]
                        self.tt(POOL, mt.t[:, :, :], eg.t[:, :, :], ms.t[:, :].unsqueeze(1).to_broadcast([128, 4, 128]),
                                ALU.mult, [eg, ms], [mt])
                        xd = xdt[i2]
                        xe = xdte[i2]
                        self.tt(POOL, xd.t[:, :].rearrange("p (a b) -> p a b", a=4),
                                xm.t[:, tb, :].rearrange("p (a b) -> p a b", a=4),
                                dtt.t[:, tb, hs].unsqueeze(2).to_broadcast([128, 4, 64]), ALU.mult, [xm, dtt], [xd])
                        self.tt(POOL, xe.t[:, :].rearrange("p (a b) -> p a b", a=4),
                                xd.t[:, :].rearrange("p (a b) -> p a b", a=4),
                                toend.t[:, tb, hs].unsqueeze(2).to_broadcast([128, 4, 64]), ALU.mult, [xd, toend], [xe])
                        py = self.pbank()
                        for r in range(4):
                            self.mm(py.t[:, r * 64:(r + 1) * 64], mt.t[:, r, :], xd.t[:, r * 64:(r + 1) * 64],
                                    True, True, [mt, xd], [py])
                        self.mm(py.t[:, 256:512], ct_all.t[:, g, tb * 128:(tb + 1) * 128], state_bf.t[:, g, :],
                                True, True, [ct_all, state_bf], [py])
                        a1 = t1[0]
                        a2 = t2[0]
                        self.tt(DVE, a1.t[:, :].rearrange("p (a b) -> p a b", a=4),
                                py.t[:, 256:512].rearrange("p (a b) -> p a b", a=4),
                                eloc.t[:, tb, hs].unsqueeze(2).to_broadcast([128, 4, 64]), ALU.mult, [py, eloc], [a1])
                        self.tt(POOL, a2.t[:, :].rearrange("p (a b) -> p a b", a=4),
                                xm.t[:, tb, :].rearrange("p (a b) -> p a b", a=4),
                                self.par("dskip", l * 32 + 4 * g, l * 32 + 4 * g + 4).unsqueeze(2).to_broadcast([128, 4, 64]),
                                ALU.mult, [xm, params], [a2])
                        self.tt(POOL, a2.t[:, :], a2.t[:, :], a1.t[:, :], ALU.add, [a1, a2], [a2])
                        ydst = (p4 if tb < 2 else p5)
                        yv = ydst.t[:, :, :].rearrange("p a b -> p (a b)")[:, (tb % 2) * 2048 + g * 256:(tb % 2) * 2048 + (g + 1) * 256]
                        self.tt(DVE, yv, py.t[:, 0:256], a2.t[:, :], ALU.add, [py, a2], [ydst])
                        pst = self.pbank()
                        self.mm(pst.t[:, 0:256], bm.t[:, tb, :], xe.t[:, :], True, True, [bm, xe], [pst])
                        self.tt(DVE, state.t[:, g, :].rearrange("p (a b) -> p a b", a=4),
                                state.t[:, g, :].rearrange("p (a b) -> p a b", a=4),
                                dk.t[:, tb, hs].unsqueeze(2).to_broadcast([128, 4, 64]), ALU.mult, [state, dk], [state])
                        self.tt(DVE, state.t[:, g, :], state.t[:, g, :], pst.t[:, 0:256], ALU.add, [state, pst], [state])
                        self.cp(ACT, state_bf.t[:, g, :], state.t[:, g, :], [state], [state_bf])
                    self.chk(9)
                p4f = p4.t[:, :, :].rearrange("p a b -> p (a b)")
                p5f = p5.t[:, :, :].rearrange("p a b -> p (a b)")
                self.dma(SP, (lambda it: (lambda e: e.dma_start(out=sp_yl[it][:, 0:4096], in_=p4f[:, 0:4096])))(it),
                         [p4], [R_yl[it]])
                self.dma(SP, (lambda it: (lambda e: e.dma_start(out=sp_yl[it][:, 4096:8192], in_=p5f[:, 0:4096])))(it),
                         [p5], [R_yl[it]])
                self.dma(SP, (lambda it: (lambda e: e.dma_start(out=sp_ct[it].rearrange("p (a b) -> p a b", a=8),
                                                                 in_=ct_all.t[:, :, :])))(it), [ct_all], [R_ct[it]])
                self.chk(10)

            self.chk(11)

            for it in range(NT):
                ts_ = it * T
                ctl, gcp, gss = gA, gA, gB
                self.dma(SP, (lambda it: (lambda e: e.dma_start(out=xT.t[:, :, :], in_=xres[:, :, it * T:(it + 1) * T])))(it),
                         [R_x[it]], [xT])
                self.dma(SP, (lambda it: (lambda e: e.dma_start(out=p4.t[:, :, :].rearrange("p a b -> p (a b)")[:, 0:4096],
                                                                 in_=sp_yl[it][:, 0:4096])))(it), [R_yl[it]], [p4])
                self.dma(SP, (lambda it: (lambda e: e.dma_start(out=p5.t[:, :, :].rearrange("p a b -> p (a b)")[:, 0:4096],
                                                                 in_=sp_yl[it][:, 4096:8192])))(it), [R_yl[it]], [p5])
                f32af = f32a.t[:, :, :].rearrange("p a b -> p (a b)")
                sqv = gB.t[:, :, :].rearrange("p a b -> p (a b)")[:, 0:2048]
                for tb in range(4):
                    zt = raw[tb % 2]
                    ztf = zt.t[:, :, :].rearrange("p a b -> p (a b)")[:, 0:2048]
                    self.dma(SP, (lambda it, tb, ztf: (lambda e: e.dma_start(out=ztf,
                                                                              in_=sp_sz[it][:, tb * 2048:(tb + 1) * 2048])))(it, tb, ztf),
                             [R_sz[it]], [zt])
                    ysrc = (p4 if tb < 2 else p5).t[:, :, :].rearrange("p a b -> p (a b)")[:, (tb % 2) * 2048:(tb % 2 + 1) * 2048]
                    ysrc_t = p4 if tb < 2 else p5
                    self.cp(POOL, f32af, ysrc, [ysrc_t], [f32a])
                    self.tt(POOL, f32af, f32af, ztf, ALU.mult, [f32a, zt], [f32a])
                    self.tt(DVE, sqv, f32af, f32af, ALU.mult, [f32a], [gB])
                    self.op(DVE, lambda e: e.tensor_reduce(out=ssq.t[:, :], in_=sqv.rearrange("p (a b) -> p a b", a=8),
                                                           axis=AX.X, op=ALU.add), [gB], [ssq])
                    self.ts(DVE, rs8.t[:, :], ssq.t[:, :], 1.0 / 256.0, ALU.mult, [ssq, params], [rs8],
                            s2=self.par("eps"), op1=ALU.add)
                    self.act(rs8.t[:, :], rs8.t[:, :], AF.Ln, [rs8], [rs8])
                    self.act(rs8.t[:, :], rs8.t[:, :], AF.Exp, [rs8], [rs8], scale=-0.5)
                    self.tt(DVE, ztf.rearrange("p (a b) -> p a b", a=8),
                            f32af.rearrange("p (a b) -> p a b", a=8),
                            rs8.t[:, :].unsqueeze(2).to_broadcast([128, 8, 256]), ALU.mult, [f32a, rs8], [zt])
                    for hf in range(2):
                        tbk = self.tbank()
                        for c8 in range(8):
                            cbk = hf * 8 + c8
                            self.tr(tbk.t[:, c8 * 128:(c8 + 1) * 128], ztf[:, cbk * 128:(cbk + 1) * 128], ident_b,
                                    [zt, cbf], [tbk])
                        ydst_t = uT if hf == 0 else p3
                        self.tt(DVE, ydst_t.t[:, :, tb * 128:(tb + 1) * 128], tbk.t[:, :].rearrange("p (a b) -> p a b", a=8),
                                self.par("nw", l * 16 + hf * 8, l * 16 + hf * 8 + 8).unsqueeze(2).to_broadcast([128, 8, 128]),
                                ALU.mult, [tbk, params], [ydst_t])
                self.chk(13)
                self.dma(SP, (lambda it: (lambda e: e.dma_start(out=gcp.t[:, :, :],
                                                                 in_=sp_gcpc[it].rearrange("p (a b) -> p a b", a=8))))(it),
                         [R_gcpc[it]], [gcp])
                self.dma(SP, (lambda it: (lambda e: e.dma_start(out=gss.t[:, :, :],
                                                                 in_=sp_gssm[it].rearrange("p (a b) -> p a b", a=8))))(it),
                         [R_gssm[it]], [gss])
                merged = p4
                for b in range(4):
                    slot = self.wnext()
                    wv = self.wview(slot, 16, 256)
                    for c2 in range(2):
                        ob = b * 2 + c2
                        pb = self.pbank()
                        for k in range(16):
                            rhs = uT.t[:, k, :] if k < 8 else p3.t[:, k - 8, :]
                            self.mm(pb.t[:, :], wv[:, k, c2 * 128:(c2 + 1) * 128], rhs, k == 0, k == 15,
                                    [slot, uT, p3], [pb])
                        ac = accf[ob % 2]
                        self.tt(DVE, ac.t[:, :], pb.t[:, :], gss.t[:, ob, :], ALU.mult, [pb, gss], [ac])
                        self.tt(POOL, merged.t[:, ob, 0:T], ac.t[:, :], gcp.t[:, ob, :], ALU.add, [ac, gcp], [merged])
                for b in range(2):
                    slot = self.wnext()
                    wv = self.wview(slot, 8, 512)
                    for c4 in range(4):
                        ob = b * 4 + c4
                        pb = self.pbank()
                        for k in range(8):
                            self.mm(pb.t[:, :], wv[:, k, c4 * 128:(c4 + 1) * 128], merged.t[:, k, 0:T], k == 0, k == 7,
                                    [slot, merged], [pb])
                        self.stt(xT.t[:, ob, :], pb.t[:, :], lay.t[:, 2, ob:ob + 1], xT.t[:, ob, :],
                                 ALU.mult, ALU.add, [pb, lay, xT], [xT])
                self.chk(14)
                norm_to_u(ts_, 3, 4)
                hid = p5
                for fg in range(4):
                    for b in range(2):
                        slot = self.wnext()
                        wv = self.wview(slot, 8, 512)
                        for c4 in range(4):
                            fb = b * 4 + c4
                            pb = self.pbank()
                            for k in range(8):
                                self.mm(pb.t[:, :], wv[:, k, c4 * 128:(c4 + 1) * 128], uT.t[:, k, :], k == 0, k == 7,
                                        [slot, uT], [pb])
                            rl = bTg[fb % 2]
                            self.act(rl.t[:, :], pb.t[:, :], AF.Relu, [pb], [rl])
                            self.tt(POOL, hid.t[:, fb, :], rl.t[:, :], rl.t[:, :], ALU.mult, [rl], [hid])
                    for b in range(2):
                        slot = self.wnext()
                        wv = self.wview(slot, 8, 512)
                        for c4 in range(4):
                            ob = b * 4 + c4
                            pb = self.pbank()
                            for k in range(8):
                                self.mm(pb.t[:, :], wv[:, k, c4 * 128:(c4 + 1) * 128], hid.t[:, k, :], k == 0, k == 7,
                                        [slot, hid], [pb])
                            self.stt(xT.t[:, ob, :], pb.t[:, :], lay.t[:, 5, ob:ob + 1],
                                     xT.t[:, ob, :], ALU.mult, ALU.add, [pb, lay, xT], [xT])
                self.dma(SP, (lambda it: (lambda e: e.dma_start(out=xres[:, :, it * T:(it + 1) * T], in_=xT.t[:, :, :])))(it),
                         [xT], [R_x[it]])

    def finish(self, env):
        (nc, xT, params, cbf, lay, f32a, p3, rstd, accf, ident_f, onesd_b, out, PE, ACT, DVE, POOL, SP, xres, R_x) = [env[k] for k in (
            "nc xT params cbf lay f32a p3 rstd accf ident_f onesd_b out PE ACT DVE POOL SP xres R_x").split()]
        self.cp(DVE, lay.t[:, 0, :], self.par("fn"), [params], [lay])
        self.op(POOL, lambda e: e.memset(lay.t[:, 1, :], 0.0), [], [lay])
        for it in range(NT):
            ts_ = it * T
            self.dma(SP, (lambda it: (lambda e: e.dma_start(out=xT.t[:, :, :], in_=xres[:, :, it * T:(it + 1) * T])))(it),
                     [R_x[it]], [xT])
            for h in range(2):
                tsl = slice(h * 256, (h + 1) * 256)
                self.act(p3.t[:, :, h * 256:(h + 1) * 256], xT.t[:, :, tsl], AF.Square, [xT], [p3])
            pb = self.pbank()
            for k in range(8):
                self.mm(pb.t[:, :], onesd_b, p3.t[:, k, :], k == 0, k == 7, [cbf, p3], [pb])
            self.act(rstd.t[:, :], pb.t[:, :], AF.Ln, [pb, params], [rstd], bias=self.par("eps"))
            self.act(rstd.t[:, :], rstd.t[:, :], AF.Exp, [rstd], [rstd], scale=-0.5)
            for h in range(2):
                tsl = slice(h * 256, (h + 1) * 256)
                self.tt(DVE, f32a.t[:, :, :], xT.t[:, :, tsl],
                        rstd.t[:, h * 256:(h + 1) * 256].unsqueeze(1).to_broadcast([128, 8, 256]),
                        ALU.mult, [xT, rstd], [f32a])
                for j in range(8):
                    self.act(f32a.t[:, j, :], f32a.t[:, j, :], AF.Identity, [f32a, lay], [f32a], scale=lay.t[:, 0, j:j + 1])
                for t2_ in range(2):
                    tb = it * 4 + h * 2 + t2_
                    for hh in range(2):
                        st = accf[hh]
                        pb2 = self.pbank()
                        for jj in range(4):
                            j = hh * 4 + jj
                            self.tr(pb2.t[:, jj * 128:(jj + 1) * 128], f32a.t[:, j, t2_ * 128:(t2_ + 1) * 128], ident_f,
                                    [f32a], [pb2])
                        self.cp(ACT if hh else DVE, st.t[:, :], pb2.t[:, :], [pb2], [st])
                        self.dma(SP, (lambda tb, hh, st: (lambda e: e.dma_start(
                            out=out[tb * 128:(tb + 1) * 128, hh * 512:(hh + 1) * 512], in_=st.t[:, :])))(tb, hh, st),
                            [st], [self.dbg_res], dsem=self.out_sem)
        SP.prog.append(("wait", self.out_sem, self.out_sem.count))

        def replay(Q):
            def body(e):
                for item in Q.prog:
                    if item[0] == "wait":
                        e.wait_ge(item[1].h, item[2])
                    elif item[0] == "op":
                        inst = item[1](e)
                        if item[2]:
                            inst.then_inc(Q.sem.h, 1)
                    elif item[0] == "cc":
                        item[1](e).then_inc(item[2].h)
                    else:
                        item[1](e).then_inc(item[2].h, 16)
            return body

        with nc.Block() as block:
            block.tensor(replay(PE))
            block.scalar(replay(ACT))
            block.vector(replay(DVE))
            block.gpsimd(replay(POOL))
            block.sync(replay(SP))
        self.stack.close()
        return nc


def _pm(a, n):
    a = np.asarray(a, np.float32)
    lead = a.shape[:-1]
    r = a.reshape(lead + (n, 128))
    return np.moveaxis(r, -1, 0)


def _host_inputs(inputs, nl=DEPTH):
    consts = np.zeros((128, 512), np.float32)
    consts[:, 0:128] = np.eye(128, dtype=np.float32)
    s = np.arange(128)
    consts[:, 128:256] = (s[:, None] <= s[None, :]).astype(np.float32)
    consts[:, 256:384] = (s[:, None] > s[None, :]).astype(np.float32)
    consts[:, 384:512] = 1.0
    per_core = []
    for k in range(NCORES):
        b, q = k, 0
        P = np.zeros((128, NPAR), np.float32)

        def put(name, arr):
            o, w = _cols[name]
            P[:, o:o + w] = np.asarray(arr, np.float32).reshape(128, w)

        put("c", _pm(inputs["c"][b], 8))
        put("b_ada", _pm(inputs["b_ada"], 48))
        put("ln1", _pm(inputs["ln1"], 8))
        put("ln2", _pm(inputs["ln2"], 8))
        put("fn", _pm(inputs["final_norm"], 8))
        put("convw", _pm(inputs["conv_w"], 8))
        put("sconvw", _pm(inputs["ssm_conv_w"], 32))
        put("sconvb", _pm(inputs["ssm_conv_b"], 32))
        put("dtb", np.broadcast_to(np.asarray(inputs["dt_bias"], np.float32).reshape(1, 128), (128, 128)))
        put("alog", np.broadcast_to(np.asarray(inputs["a_log"], np.float32).reshape(1, 128), (128, 128)))
        put("dskip", np.broadcast_to(np.asarray(inputs["d_skip"], np.float32).reshape(1, 128), (128, 128)))
        put("nw", _pm(inputs["ssm_norm_w"], 16))
        sel = np.zeros(8, np.float32)
        mk = np.zeros(8, np.float32)
        bt = np.zeros((8, 8), np.float32)
        put("sel", np.broadcast_to(sel.reshape(1, 8), (128, 8)))
        put("mk", np.broadcast_to(mk.reshape(1, 8), (128, 8)))
        put("bt", np.broadcast_to(bt.reshape(1, 64), (128, 64)))
        put("one", np.ones((128, 1), np.float32))
        put("eps", np.full((128, 1), EPS, np.float32))
        m = {
            "x": np.ascontiguousarray(np.asarray(inputs["x"], np.float32)[b, q * NTOK:(q + 1) * NTOK, :]),
            "params": P,
            "consts": consts,
        }
        for wn in ("w_ada", "w_in", "w_conv_out", "w_ssm_out", "w_o", "w_up", "w_down"):
            m[wn] = np.ascontiguousarray(np.asarray(inputs[wn], np.float32)[:max(nl, 1)])
        per_core.append(m)
    return per_core


_CACHE = {}


def _run(inputs, nl=DEPTH, debug=(), lim=10 ** 9):
    key = (nl, tuple(debug), lim)
    if key not in _CACHE:
        bld = Builder(nl, debug, lim)
        _CACHE[key] = (bld.build(), bld)
    nc, bld = _CACHE[key]
    in_maps = _host_inputs(inputs, nl)
    res = run_bass_kernel_spmd(nc, in_maps, core_ids=list(range(NCORES)))
    out = np.empty((2, SEQ, D), np.float32)
    for k in range(NCORES):
        out[k, :, :] = res.results[k]["out"]
    return out, res


def kernel(**inputs):
    out, _ = _run(inputs, DEPTH)
    return out
```

```python
import contextlib
import numpy as np
import concourse.bass as bass
import concourse.mybir as mybir
from concourse.bass_utils import run_bass_kernel_spmd

F32 = mybir.dt.float32
BF16 = mybir.dt.bfloat16
AF = mybir.ActivationFunctionType
ALU = mybir.AluOpType
AX = mybir.AxisListType

NCORES = 2
D = 1024
SEQ = 8192
NTOK = 8192
T = 512
NT = NTOK // T
DEPTH = 4
DPROJ = 11296
OFF_GL, OFF_CB, OFF_CC, OFF_CX, OFF_Z, OFF_XBC, OFF_DT = 0, 2048, 3072, 4096, 5120, 7168, 11264
EPS = 1e-6
NRING = 6

_cols = {}
_off = 0
for _n, _w in [("c", 8), ("b_ada", 192), ("ln1", 32), ("ln2", 32), ("fn", 8), ("convw", 96),
               ("sconvw", 512), ("sconvb", 128), ("dtb", 128), ("alog", 128), ("dskip", 128),
               ("nw", 64), ("sel", 8), ("mk", 8), ("bt", 64), ("one", 1), ("eps", 1)]:
    _cols[_n] = (_off, _w)
    _off += _w
NPAR = _off


class Res:
    __slots__ = ("name", "w", "r")

    def __init__(self, name):
        self.name = name
        self.w = None
        self.r = {}


class Sem:
    def __init__(self, handle, name):
        self.h = handle
        self.name = name
        self.count = 0


class Tile:
    def __init__(self, t, name):
        self.t = t
        self.res = Res(name)

    def __getitem__(self, idx):
        return self.t[idx]


class View:
    def __init__(self, tile, ap):
        self.t = ap
        self.res = tile.res

    def __getitem__(self, idx):
        return self.t[idx]


class _Stop(Exception):
    pass


class Queue:
    def __init__(self, name, sem, is_pe=False):
        self.name = name
        self.sem = sem
        self.known = {}
        self.prog = []
        self.is_pe = is_pe


def _res(x):
    return x.res if isinstance(x, (Tile, View)) else x


class Builder:
    def __init__(self, nl, debug=(), lim=10 ** 9):
        self.nl = nl
        self.lim = lim
        self.debug = tuple(debug)
        self.nc = bass.Bass("TRN2", target_bir_lowering=False)
        self.stack = contextlib.ExitStack()
        self.dbg_out = {}
        self.n_cc = 0

    def sem(self, name):
        return Sem(self.stack.enter_context(self.nc.semaphore(name)), name)

    def sb(self, name, shape, dt):
        return Tile(self.stack.enter_context(self.nc.sbuf_tensor("sb_" + name, list(shape), dt)), name)

    def ps(self, name, shape, dt):
        return Tile(self.stack.enter_context(self.nc.psum_tensor("ps_" + name, list(shape), dt)), name)

    def _deps(self, E, reads, writes):
        need = {}

        def add(tok):
            if tok is None:
                return
            s, v = tok
            if need.get(s, 0) < v:
                need[s] = v

        for r in reads:
            add(r.w)
        for r in writes:
            add(r.w)
            for s, v in r.r.items():
                add((s, v))
        for s, v in need.items():
            if E.is_pe and s is E.sem:
                continue
            if E.known.get(s, 0) < v:
                E.known[s] = v
                E.prog.append(("wait", s, v))

    def _mark(self, tok, reads, writes):
        s, v = tok
        for r in reads:
            if r.r.get(s, 0) < v:
                r.r[s] = v
        for r in writes:
            r.w = tok
            r.r = {}

    def op(self, E, fn, reads=(), writes=(), inc=True):
        reads = [_res(x) for x in reads]
        writes = [_res(x) for x in writes]
        self._deps(E, reads, writes)
        if inc:
            E.sem.count += 1
            tok = (E.sem, E.sem.count)
        else:
            tok = (E.sem, E.sem.count + 1)
        E.prog.append(("op", fn, inc))
        self._mark(tok, reads, writes)

    def dma(self, E, fn, reads=(), writes=(), dsem=None):
        reads = [_res(x) for x in reads]
        writes = [_res(x) for x in writes]
        if dsem is None:
            pool = self.dsems[E.name]
            dsem = pool[self.dsem_rr[E.name] % len(pool)]
            self.dsem_rr[E.name] += 1
        if E.known.get(dsem, 0) < dsem.count:
            E.known[dsem] = dsem.count
            E.prog.append(("wait", dsem, dsem.count))
        self._deps(E, reads, writes)
        dsem.count += 16
        tok = (dsem, dsem.count)
        E.prog.append(("dma", fn, dsem))
        self._mark(tok, reads, writes)
        return tok

    def collective(self, cin, cout, r_in, r_out):
        POOL = self.POOL
        for sm in self.wsems + self.dsems["pool"]:
            if POOL.known.get(sm, 0) < sm.count:
                POOL.known[sm] = sm.count
                POOL.prog.append(("wait", sm, sm.count))
        ccs = self.sem("cc%d" % self.n_cc)
        self.n_cc += 1
        reads = [r_in]
        writes = [r_out]
        self._deps(POOL, reads, writes)
        POOL.prog.append(("cc", lambda e: e.collective_compute("AllGather", ALU.bypass,
                                                               replica_groups=[list(range(NCORES))],
                                                               ins=[cin], outs=[cout]), ccs))
        self._mark((ccs, 1), reads, writes)
        POOL.known[ccs] = 1
        POOL.prog.append(("wait", ccs, 1))

    def chk(self, n):
        if n >= self.lim:
            raise _Stop()

    def mm(self, out, lhsT, rhs, start, stop, reads, writes):
        self.op(self.PE, lambda e: e.matmul(out, lhsT=lhsT, rhs=rhs, start=start, stop=stop),
                reads, writes, inc=stop)

    def tr(self, out, in_, ident, reads, writes):
        self.op(self.PE, lambda e: e.transpose(out, in_, ident), list(reads) + [self.consts], writes)

    def act(self, out, in_, func, reads, writes, bias=None, scale=None):
        kw = {}
        if bias is not None:
            kw["bias"] = bias
        if scale is not None:
            kw["scale"] = scale
        self.op(self.ACT, lambda e: e.activation(out=out, in_=in_, func=func, **kw), reads, writes)

    def tt(self, E, out, in0, in1, op, reads, writes):
        self.op(E, lambda e: e.tensor_tensor(out=out, in0=in0, in1=in1, op=op), reads, writes)

    def ts(self, E, out, in0, s1, op0, reads, writes, s2=None, op1=None):
        if op1 is None:
            self.op(E, lambda e: e.tensor_scalar(out=out, in0=in0, scalar1=s1, scalar2=None, op0=op0),
                    reads, writes)
        else:
            self.op(E, lambda e: e.tensor_scalar(out=out, in0=in0, scalar1=s1, scalar2=s2, op0=op0, op1=op1),
                    reads, writes)

    def stt(self, out, in0, scalar, in1, op0, op1, reads, writes):
        self.op(self.DVE, lambda e: e.scalar_tensor_tensor(out=out, in0=in0, scalar=scalar, in1=in1,
                                                           op0=op0, op1=op1), reads, writes)

    def cp(self, E, out, in_, reads, writes):
        if E is self.ACT:
            self.op(E, lambda e: e.activation(out=out, in_=in_, func=AF.Copy), reads, writes)
        else:
            self.op(E, lambda e: e.tensor_copy(out=out, in_=in_), reads, writes)

    def pbank(self):
        b = self.pbanks[self.pb_rr % len(self.pbanks)]
        self.pb_rr += 1
        return b

    def tbank(self):
        b = self.tbanks[self.tb_rr % len(self.tbanks)]
        self.tb_rr += 1
        return b

    def par(self, name, lo=0, hi=None):
        o, w = _cols[name]
        hi = w if hi is None else hi
        return self.params[:, o + lo:o + hi]

    def dbg(self, name, tile, ap, shape, dt=F32):
        if name not in self.debug:
            return
        d = self.nc.dram_tensor("dbg_" + name, list(shape), dt, kind="ExternalOutput").ap()
        self.dbg_out[name] = d
        self.dma(self.SP, lambda e: e.dma_start(out=d, in_=ap), [tile], [self.dbg_res], dsem=self.out_sem)

    def wnext(self):
        while self.w_issued < len(self.wplan) and self.w_issued < self.w_used + NRING:
            pieces = self.wplan[self.w_issued]
            slot = self.ring[self.w_issued % NRING]
            dsem = self.wsems[self.w_issued % NRING]
            for (src, dst_fn) in pieces:
                dst = dst_fn(slot)
                self.dma(self.POOL, (lambda s, d: (lambda e: e.dma_start(out=d, in_=s)))(src, dst),
                         [], [slot], dsem=dsem)
            self.w_issued += 1
        slot = self.ring[self.w_used % NRING]
        self.w_used += 1
        return slot

    @staticmethod
    def wview(slot, kc, w):
        return slot.t[:, 0:kc * w].rearrange("p (j c) -> p j c", j=kc)

    def plan_weights(self, W):
        plan = []

        def blk(mat, c0, w, kc=8, r0=0):
            src = mat[r0:r0 + kc * 128, c0:c0 + w].rearrange("(j p) c -> p j c", p=128)
            plan.append([(src, lambda s, kc=kc, w=w: Builder.wview(s, kc, w))])

        for l in range(self.nl):
            for b in range(12):
                blk(W["w_ada"][l], b * 512, 512)
        for l in range(self.nl):
            win = W["w_in"][l]
            for it in range(NT):
                for b in range(4):
                    blk(win, OFF_GL + b * 512, 512)
                for off in (OFF_CX, OFF_CC, OFF_CB):
                    for b in range(2):
                        blk(win, off + b * 512, 512)
                for b in range(2):
                    blk(W["w_conv_out"][l], b * 512, 512)
                for b in range(4):
                    blk(win, OFF_Z + b * 512, 512)
                blk(win, OFF_DT + 32 - 512, 512)
                for g in range(8):
                    pcs = []
                    for (c0, w, d0) in ((OFF_XBC + 256 * g, 256, 0), (OFF_XBC + 2048 + 128 * g, 128, 256),
                                        (OFF_XBC + 3072 + 128 * g, 128, 384)):
                        src = win[:, c0:c0 + w].rearrange("(j p) c -> p j c", p=128)
                        pcs.append((src, lambda s, d0=d0, w=w: Builder.wview(s, 8, 512)[:, :, d0:d0 + w]))
                    plan.append(pcs)
            for it in range(NT):
                for b in range(4):
                    blk(W["w_ssm_out"][l], b * 256, 256, kc=16)
                for b in range(2):
                    blk(W["w_o"][l], b * 512, 512)
                for fg in range(4):
                    for b in range(2):
                        blk(W["w_up"][l], fg * 1024 + b * 512, 512)
                    for b in range(2):
                        blk(W["w_down"][l], b * 512, 512, r0=fg * 1024)
        self.wplan = plan
        self.w_issued = 0
        self.w_used = 0

    def build(self):
        nc = self.nc
        nl = self.nl
        ins = {}

        def din(name, shape, dt=F32):
            t = nc.dram_tensor(name, list(shape), dt, kind="ExternalInput").ap()
            ins[name] = t
            return t

        x_in = din("x", [NTOK, D])
        params_in = din("params", [128, NPAR])
        consts_in = din("consts", [128, 512])
        W = {}
        W["w_ada"] = din("w_ada", [max(nl, 1), D, 6 * D])
        W["w_in"] = din("w_in", [max(nl, 1), D, DPROJ])
        W["w_conv_out"] = din("w_conv_out", [max(nl, 1), D, D])
        W["w_ssm_out"] = din("w_ssm_out", [max(nl, 1), 2 * D, D])
        W["w_o"] = din("w_o", [max(nl, 1), D, D])
        W["w_up"] = din("w_up", [max(nl, 1), D, 4 * D])
        W["w_down"] = din("w_down", [max(nl, 1), 4 * D, D])
        out = nc.dram_tensor("out", [NTOK, D], F32, kind="ExternalOutput").ap()

        sp_gssm = nc.dram_tensor("sp_gssm", [NT, 128, 8 * T], BF16).ap()
        sp_gcpc = nc.dram_tensor("sp_gcpc", [NT, 128, 8 * T], BF16).ap()
        sp_sz = nc.dram_tensor("sp_sz", [NT, 128, 4 * 2048], BF16).ap()
        sp_yl = nc.dram_tensor("sp_yl", [NT, 128, 4 * 2048], BF16).ap()
        sp_ct = nc.dram_tensor("sp_ct", [NT, 128, 8 * T], BF16).ap()
        cc_st_in = nc.dram_tensor("cc_st_in", [128, 2048 + 32], F32).ap()
        cc_st_out = nc.dram_tensor("cc_st_out", [NCORES * 128, 2048 + 32], F32).ap()
        cc_h_in = nc.dram_tensor("cc_h_in", [128, 112], F32).ap()
        cc_h_out = nc.dram_tensor("cc_h_out", [NCORES * 128, 112], F32).ap()
        xres = nc.dram_tensor("xres", [128, 8, NTOK], F32).ap()
        R_x = [Res("xres%d" % i) for i in range(NT)]
        R_gssm = [Res("sp_gssm%d" % i) for i in range(NT)]
        R_gcpc = [Res("sp_gcpc%d" % i) for i in range(NT)]
        R_sz = [Res("sp_sz%d" % i) for i in range(NT)]
        R_yl = [Res("sp_yl%d" % i) for i in range(NT)]
        R_ct = [Res("sp_ct%d" % i) for i in range(NT)]
        R_ccst_in, R_ccst_out = Res("ccsti"), Res("ccsto")
        R_cch_in, R_cch_out = Res("cchi"), Res("ccho")
        self.dbg_res = Res("dbg")

        self.PE = Queue("pe", self.sem("s_pe"), is_pe=True)
        self.ACT = Queue("act", self.sem("s_act"))
        self.DVE = Queue("dve", self.sem("s_dve"))
        self.POOL = Queue("pool", self.sem("s_pool"))
        self.SP = Queue("sp", self.sem("s_sp"))
        PE, ACT, DVE, POOL, SP = self.PE, self.ACT, self.DVE, self.POOL, self.SP
        self.dsems = {"sp": [self.sem("d_sp%d" % i) for i in range(16)],
                      "pool": [self.sem("d_pl%d" % i) for i in range(4)]}
        self.dsem_rr = {"sp": 0, "pool": 0}
        self.wsems = [self.sem("d_w%d" % i) for i in range(NRING)]
        self.out_sem = self.sem("d_out")

        xT = self.sb("xT", [128, 8, T], F32)
        self.ring = [self.sb("ring%d" % i, [128, 4096], BF16) for i in range(NRING)]
        self.params = None
        params = self.sb("params", [128, NPAR], F32)
        self.params = params.t
        self.consts = consts = self.sb("consts", [128, 512], F32)
        cbf = self.sb("cbf", [128, 640], BF16)
        modT = self.sb("modT", [128, 192], F32)
        lay = self.sb("lay", [128, 6, 8], F32)
        arep = self.sb("arep", [128, 32], F32)
        cact = self.sb("cact", [128, 8], BF16)
        state = self.sb("state", [128, 8, 256], F32)
        state_bf = self.sb("state_bf", [128, 8, 256], BF16)
        sstart = state_bf
        ccx_prev = self.sb("ccx_prev", [128, 8, 2], BF16)
        xbc_prev = self.sb("xbc_prev", [128, 32, 3], BF16)
        eseg = self.sb("eseg", [128, 4, 32], F32)
        arun = self.sb("arun", [128, 32], F32)
        f32a = self.sb("f32a", [128, 8, 256], F32)
        uT = self.sb("uT", [128, 8, T], BF16)
        rstd = self.sb("rstd", [128, T], F32)
        gA = self.sb("gA", [128, 8, T], BF16)
        gB = self.sb("gB", [128, 8, T], BF16)
        p3 = self.sb("p3", [128, 8, T], BF16)
        p4 = self.sb("p4", [128, 8, T + 2], BF16)
        p5 = self.sb("p5", [128, 8, T], BF16)
        zb = [self.sb("zb%d" % i, [128, 512], BF16) for i in range(2)]
        raw = [self.sb("raw%d" % i, [128, 4, T + 3], BF16) for i in range(2)]
        accf = [self.sb("accf%d" % i, [128, T], F32) for i in range(2)]
        xTg = [self.sb("xTg%d" % i, [128, 2, T], BF16) for i in range(1)]
        bTg = [self.sb("bTg%d" % i, [128, T], BF16) for i in range(2)]
        xtm = [self.sb("xtm%d" % i, [128, 4, 256], BF16) for i in range(1)]
        btm = [self.sb("btm%d" % i, [128, 4, 128], BF16) for i in range(1)]
        dtt = self.sb("dtt", [128, 4, 32], F32)
        att = self.sb("att", [128, 4, 32], F32)
        a3 = [self.sb("a3_%d" % i, [128, 4, 32], BF16) for i in range(3)]
        acum = self.sb("acum", [128, 4, 32], F32)
        eloc = self.sb("eloc", [128, 4, 32], F32)
        toend = self.sb("toend", [128, 4, 32], F32)
        dk = self.sb("dk", [128, 4, 32], F32)
        tmp32 = self.sb("tmp32", [128, 4, 32], F32)
        arhs = [self.sb("arhs%d" % i, [128, 4, 128], BF16) for i in range(2)]
        eseg_g = [self.sb("esg%d" % i, [128, 4, 128], BF16) for i in range(2)]
        msc = [self.sb("msc%d" % i, [128, 128], BF16) for i in range(2)]
        mT = [self.sb("mT%d" % i, [128, 4, 128], BF16) for i in range(2)]
        xdt = [self.sb("xdt%d" % i, [128, 256], BF16) for i in range(2)]
        xdte = [self.sb("xdte%d" % i, [128, 256], BF16) for i in range(2)]
        t1 = [self.sb("t1_%d" % i, [128, 256], F32) for i in range(1)]
        t2 = [self.sb("t2_%d" % i, [128, 256], F32) for i in range(1)]
        hsel = View(t2[0], t2[0].t[:, 0:112])
        atall = View(t1[0], t1[0].t[:, :].rearrange("p (a b) -> p a b", a=8))
        coef = View(t2[0], t2[0].t[:, :].rearrange("p (a b) -> p a b", a=8))
        u3 = self.sb("u3", [128, 8, 4], BF16)
        sm3 = self.sb("sm3", [128, 8, 4], F32)
        r3 = self.sb("r3", [128, 4], F32)
        hraw = View(t1[0], t1[0].t[:, 0:144].rearrange("p (a b) -> p a b", b=3))
        ssq = self.sb("ssq", [128, 8], F32)
        rs8 = self.sb("rs8", [128, 8], F32)

        self.pbanks = [self.ps("pb%d" % i, [128, 512], F32) for i in range(6)]
        self.tbanks = [self.ps("tbk%d" % i, [128, 1024], BF16) for i in range(2)]
        self.pb_rr = 0
        self.tb_rr = 0

        ident_f = consts.t[:, 0:128]
        triu_f = consts.t[:, 128:256]
        lmat_f = consts.t[:, 256:384]
        ones_f = consts.t[:, 384:512]
        ident_b = cbf.t[:, 0:128]
        onesd_b = cbf.t[:, 128:256]
        triu_b = cbf.t[:, 256:384]
        lmat_b = cbf.t[:, 384:512]
        ones_b = cbf.t[:, 512:640]

        self.plan_weights(W)
        HSMV = f32a.t[:, :, :].rearrange("p a b -> p (a b)")[:, 0:896].rearrange("p (a b) -> p a b", a=8)

        self.dma(SP, lambda e: e.dma_start(out=params.t[:], in_=params_in), [], [params])
        self.dma(SP, lambda e: e.dma_start(out=consts.t[:], in_=consts_in), [], [consts])
        self.cp(DVE, ident_b, ident_f, [consts], [cbf])
        self.ts(DVE, onesd_b, ones_f, 1.0 / 1024.0, ALU.mult, [consts, cbf], [cbf])
        self.cp(DVE, cbf.t[:, 256:640], consts.t[:, 128:512], [consts, cbf], [cbf])
        for it in range(NT):
            for t4 in range(4):
                tb = it * 4 + t4
                for h in range(2):
                    st = accf[(tb * 2 + h) % 2]
                    self.dma(SP, (lambda tb, h, st: (lambda e: e.dma_start(
                        out=st.t[:, :], in_=x_in[tb * 128:(tb + 1) * 128, h * 512:(h + 1) * 512])))(tb, h, st), [], [st])
                    pb = self.pbank()
                    for jj in range(4):
                        self.tr(pb.t[:, jj * 128:(jj + 1) * 128], st.t[:, jj * 128:(jj + 1) * 128], ident_f, [st], [pb])
                    self.cp(ACT if h else DVE, xT.t[:, h * 4:(h + 1) * 4, t4 * 128:(t4 + 1) * 128],
                            pb.t[:].rearrange("p (a b) -> p a b", a=4), [pb], [xT])
            self.dma(SP, (lambda it: (lambda e: e.dma_start(out=xres[:, :, it * T:(it + 1) * T], in_=xT.t[:, :, :])))(it),
                     [xT], [R_x[it]])
        self.act(cact.t[:], self.par("c"), AF.Silu, [params], [cact])
        if nl > 0:
            pmod = self.pbank()
            for l in range(nl):
                for b in range(12):
                    slot = self.wnext()
                    wv = self.wview(slot, 8, 512)
                    for cb_ in range(4):
                        col = l * 48 + b * 4 + cb_
                        for k in range(8):
                            self.mm(pmod.t[:, col:col + 1], wv[:, k, cb_ * 128:(cb_ + 1) * 128], cact.t[:, k:k + 1],
                                    k == 0, k == 7, [slot, cact], [pmod])
            self.tt(DVE, modT.t[:, 0:48 * nl], pmod.t[:, 0:48 * nl], self.par("b_ada", 0, 48 * nl), ALU.add,
                    [pmod, params], [modT])

        def norm_to_u(ts_, gcol, shcol):
            for h in range(2):
                tsl = slice(h * 256, (h + 1) * 256)
                self.act(p3.t[:, :, h * 256:(h + 1) * 256], xT.t[:, :, tsl], AF.Square, [xT], [p3])
            pb = self.pbank()
            for k in range(8):
                self.mm(pb.t[:, :], onesd_b, p3.t[:, k, :], k == 0, k == 7, [cbf, p3], [pb])
            self.act(rstd.t[:, :], pb.t[:, :], AF.Ln, [pb, params], [rstd], bias=self.par("eps"))
            self.act(rstd.t[:, :], rstd.t[:, :], AF.Exp, [rstd], [rstd], scale=-0.5)
            for h in range(2):
                tsl = slice(h * 256, (h + 1) * 256)
                self.tt(DVE, f32a.t[:, :, :], xT.t[:, :, tsl],
                        rstd.t[:, h * 256:(h + 1) * 256].unsqueeze(1).to_broadcast([128, 8, 256]),
                        ALU.mult, [xT, rstd], [f32a])
                for j in range(8):
                    self.act(uT.t[:, j, h * 256:(h + 1) * 256], f32a.t[:, j, :], AF.Identity, [f32a, lay], [uT],
                             bias=lay.t[:, shcol, j:j + 1], scale=lay.t[:, gcol, j:j + 1])

        try:
            self.layers(locals())
        except _Stop:
            for i in range(NRING):
                if POOL.known.get(self.wsems[i], 0) < self.wsems[i].count:
                    POOL.prog.append(("wait", self.wsems[i], self.wsems[i].count))
        return self.finish(locals())

    def layers(self, env):
        (nl, xT, params, consts, cbf, modT, lay, arep, cact, state, state_bf, sstart, ccx_prev, xbc_prev, eseg, arun,
         f32a, uT, rstd, gA, gB, p3, p4, p5, zb, raw, accf, xTg, bTg, xtm, btm, dtt, att, a3, acum, eloc, toend, dk, tmp32,
         arhs, eseg_g, msc, mT, xdt, xdte, t1, t2, hsel, atall, coef, u3, sm3, r3, hraw, ssq, rs8,
         ident_f, triu_f, lmat_f, ones_f, ident_b, onesd_b, triu_b, lmat_b, ones_b, HSMV, norm_to_u,
         sp_gssm, sp_gcpc, sp_sz, sp_yl, sp_ct, cc_st_in, cc_st_out, cc_h_in, cc_h_out,
         R_gssm, R_gcpc, R_sz, R_yl, R_ct, R_ccst_in, R_ccst_out, R_cch_in, R_cch_out,
         PE, ACT, DVE, POOL, SP, xres, R_x) = [env[k] for k in (
            "nl xT params consts cbf modT lay arep cact state state_bf sstart ccx_prev xbc_prev eseg arun "
            "f32a uT rstd gA gB p3 p4 p5 zb raw accf xTg bTg xtm btm dtt att a3 acum eloc toend dk tmp32 "
            "arhs eseg_g msc mT xdt xdte t1 t2 hsel atall coef u3 sm3 r3 hraw ssq rs8 "
            "ident_f triu_f lmat_f ones_f ident_b onesd_b triu_b lmat_b ones_b HSMV norm_to_u "
            "sp_gssm sp_gcpc sp_sz sp_yl sp_ct cc_st_in cc_st_out cc_h_in cc_h_out "
            "R_gssm R_gcpc R_sz R_yl R_ct R_ccst_in R_ccst_out R_cch_in R_cch_out "
            "PE ACT DVE POOL SP xres R_x").split()]
        for l in range(nl):
            mo = l * 48
            self.stt(lay.t[:, 0, :], modT.t[:, mo + 8:mo + 16], 1.0, self.par("ln1", l * 8, l * 8 + 8),
                     ALU.add, ALU.mult, [modT, params], [lay])
            self.cp(DVE, lay.t[:, 1, :], modT.t[:, mo:mo + 8], [modT], [lay])
            self.cp(DVE, lay.t[:, 2, :], modT.t[:, mo + 16:mo + 24], [modT], [lay])
            self.stt(lay.t[:, 3, :], modT.t[:, mo + 32:mo + 40], 1.0, self.par("ln2", l * 8, l * 8 + 8),
                     ALU.add, ALU.mult, [modT, params], [lay])
            self.cp(DVE, lay.t[:, 4, :], modT.t[:, mo + 24:mo + 32], [modT], [lay])
            self.cp(DVE, lay.t[:, 5, :], modT.t[:, mo + 40:mo + 48], [modT], [lay])
            self.act(arep.t[:, :], self.par("alog", l * 32, l * 32 + 32), AF.Exp, [params], [arep])
            self.ts(DVE, arep.t[:, :], arep.t[:, :], -1.0, ALU.mult, [arep], [arep])
            self.op(POOL, lambda e: e.memset(state.t[:], 0.0), [], [state])
            self.op(POOL, lambda e: e.memset(state_bf.t[:], 0.0), [], [state_bf])
            self.op(POOL, lambda e: e.memset(arun.t[:], 0.0), [], [arun])
            self.chk(1)

            cw = lambda k, j: self.par("convw", l * 24 + k * 8 + j, l * 24 + k * 8 + j + 1)
            sw = lambda k, i: self.par("sconvw", l * 128 + k * 32 + i, l * 128 + k * 32 + i + 1)
            sbias = lambda i: self.par("sconvb", l * 32 + i, l * 32 + i + 1)

            self.op(POOL, lambda e: e.memset(xbc_prev.t[:], 0.0), [], [xbc_prev])
            self.op(POOL, lambda e: e.memset(ccx_prev.t[:], 0.0), [], [ccx_prev])
            self.chk(3)

            for it in range(NT):
                ts_ = it * T
                self.dma(SP, (lambda it: (lambda e: e.dma_start(out=xT.t[:, :, :], in_=xres[:, :, it * T:(it + 1) * T])))(it),
                         [R_x[it]], [xT])
                norm_to_u(ts_, 0, 1)
                self.chk(4)
                for b in range(4):
                    slot = self.wnext()
                    wv = self.wview(slot, 8, 512)
                    for c4 in range(4):
                        gb = b * 4 + c4
                        pb = self.pbank()
                        for k in range(8):
                            self.mm(pb.t[:, :], wv[:, k, c4 * 128:(c4 + 1) * 128], uT.t[:, k, :], k == 0, k == 7,
                                    [slot, uT], [pb])
                        dst = gA if gb < 8 else gB
                        self.act(dst.t[:, gb % 8, :], pb.t[:, :], AF.Sigmoid, [pb], [dst])
                self.dma(SP, (lambda it: (lambda e: e.dma_start(out=sp_gssm[it].rearrange("p (a b) -> p a b", a=8),
                                                                 in_=gB.t[:, :, :])))(it), [gB], [R_gssm[it]])
                self.chk(5)
                self.cp(POOL, p4.t[:, :, 0:2], ccx_prev.t[:, :, :], [ccx_prev], [p4])
                for fam in range(3):
                    for b in range(2):
                        slot = self.wnext()
                        wv = self.wview(slot, 8, 512)
                        for c4 in range(4):
                            j = b * 4 + c4
                            pb = self.pbank()
                            for k in range(8):
                                self.mm(pb.t[:, :], wv[:, k, c4 * 128:(c4 + 1) * 128], uT.t[:, k, :], k == 0, k == 7,
                                        [slot, uT], [pb])
                            if fam == 0:
                                self.act(p3.t[:, j, :], pb.t[:, :], AF.Copy, [pb], [p3])
                            elif fam == 1:
                                self.tt(DVE, p4.t[:, j, 2:T + 2], pb.t[:, :], p3.t[:, j, :], ALU.mult, [pb, p3], [p4])
                            else:
                                self.act(p3.t[:, j, :], pb.t[:, :], AF.Copy, [pb], [p3])
                self.cp(POOL, ccx_prev.t[:, :, :], p4.t[:, :, T:T + 2], [p4], [ccx_prev])
                for j in range(8):
                    ac = accf[j % 2]
                    self.ts(DVE, ac.t[:, :], p4.t[:, j, 0:T], cw(0, j), ALU.mult, [p4, params], [ac])
                    self.stt(ac.t[:, :], p4.t[:, j, 1:T + 1], cw(1, j), ac.t[:, :], ALU.mult, ALU.add,
                             [p4, params, ac], [ac])
                    self.stt(ac.t[:, :], p4.t[:, j, 2:T + 2], cw(2, j), ac.t[:, :], ALU.mult, ALU.add,
                             [p4, params, ac], [ac])
                    self.tt(DVE, p5.t[:, j, :], ac.t[:, :], p3.t[:, j, :], ALU.mult, [ac, p3], [p5])
                for b in range(2):
                    slot = self.wnext()
                    wv = self.wview(slot, 8, 512)
                    for c4 in range(4):
                        ob = b * 4 + c4
                        pb = self.pbank()
                        for k in range(8):
                            self.mm(pb.t[:, :], wv[:, k, c4 * 128:(c4 + 1) * 128], p5.t[:, k, :], k == 0, k == 7,
                                    [slot, p5], [pb])
                        self.tt(DVE, gB.t[:, ob, :], pb.t[:, :], gA.t[:, ob, :], ALU.mult, [pb, gA], [gB])
                self.dma(SP, (lambda it: (lambda e: e.dma_start(out=sp_gcpc[it].rearrange("p (a b) -> p a b", a=8),
                                                                 in_=gB.t[:, :, :])))(it), [gB], [R_gcpc[it]])
                self.chk(6)
                for zc in range(4):
                    slot = self.wnext()
                    wv = self.wview(slot, 8, 512)
                    for tb in range(4):
                        pb = self.pbank()
                        for k in range(8):
                            self.mm(pb.t[:, :], uT.t[:, k, tb * 128:(tb + 1) * 128], wv[:, k, :], k == 0, k == 7,
                                    [slot, uT], [pb])
                        zt = zb[(zc * 4 + tb) % 2]
                        self.act(zt.t[:, 0:512], pb.t[:, :], AF.Silu, [pb], [zt])
                        self.dma(SP, (lambda it, tb, zc, zt: (lambda e: e.dma_start(
                            out=sp_sz[it][:, tb * 2048 + zc * 512: tb * 2048 + (zc + 1) * 512],
                            in_=zt.t[:, 0:512])))(it, tb, zc, zt), [zt], [R_sz[it]])
                self.chk(7)
                slot = self.wnext()
                wv = self.wview(slot, 8, 512)
                pb = self.pbank()
                for tb in range(4):
                    for k in range(8):
                        self.mm(pb.t[:, tb * 32:(tb + 1) * 32], uT.t[:, k, tb * 128:(tb + 1) * 128], wv[:, k, 480:512],
                                k == 0, k == 7, [slot, uT], [pb])
                self.tt(DVE, dtt.t[:, :, :], pb.t[:, 0:128].rearrange("p (a b) -> p a b", a=4),
                        self.par("dtb", l * 32, l * 32 + 32).unsqueeze(1).to_broadcast([128, 4, 32]), ALU.add,
                        [pb, params], [dtt])
                self.act(dtt.t[:, :, :], dtt.t[:, :, :], AF.Exp, [dtt], [dtt])
                self.act(dtt.t[:, :, :], dtt.t[:, :, :], AF.Ln, [dtt, params], [dtt], bias=self.par("one"))
                self.tt(DVE, att.t[:, :, :], dtt.t[:, :, :], arep.t[:, :].unsqueeze(1).to_broadcast([128, 4, 32]),
                        ALU.mult, [dtt, arep], [att])
                self.chk(7.1)
                self.cp(DVE, a3[0].t[:, :, :], att.t[:, :, :], [att], [a3[0]])
                self.tt(DVE, tmp32.t[:, :, :], att.t[:, :, :], a3[0].t[:, :, :], ALU.subtract, [att, a3[0]], [tmp32])
                self.cp(DVE, a3[1].t[:, :, :], tmp32.t[:, :, :], [tmp32], [a3[1]])
                self.tt(DVE, tmp32.t[:, :, :], tmp32.t[:, :, :], a3[1].t[:, :, :], ALU.subtract, [tmp32, a3[1]], [tmp32])
                self.cp(DVE, a3[2].t[:, :, :], tmp32.t[:, :, :], [tmp32], [a3[2]])
                self.chk(7.2)
                for tb in range(4):
                    pb = self.pbank()
                    for xi in range(3):
                        self.mm(pb.t[:, 0:32], triu_b, a3[xi].t[:, tb, :], xi == 0, xi == 2, [cbf, a3[xi]], [pb])
                    self.cp(DVE, acum.t[:, tb, :], pb.t[:, 0:32], [pb], [acum])
                    pb2 = self.pbank()
                    for xi in range(3):
                        self.mm(pb2.t[:, 0:32], ones_b, a3[xi].t[:, tb, :], xi == 0, xi == 2, [cbf, a3[xi]], [pb2])
                    self.cp(DVE, tmp32.t[:, tb, :], pb2.t[:, 0:32], [pb2], [tmp32])
                    self.act(eloc.t[:, tb, :], acum.t[:, tb, :], AF.Exp, [acum], [eloc])
                    self.act(dk.t[:, tb, :], tmp32.t[:, tb, :], AF.Exp, [tmp32], [dk])
                    self.tt(DVE, toend.t[:, tb, :], tmp32.t[:, tb, :], acum.t[:, tb, :], ALU.subtract, [tmp32, acum], [toend])
                    self.act(toend.t[:, tb, :], toend.t[:, tb, :], AF.Exp, [toend], [toend])
                    self.tt(DVE, arun.t[:, :], arun.t[:, :], tmp32.t[:, tb, :], ALU.add, [arun, tmp32], [arun])
                self.chk(8)
                ct_all = gA
                for g in range(8):
                    slot = self.wnext()
                    wv = self.wview(slot, 8, 512)
                    rw = raw[g % 2]
                    idxs = (2 * g, 2 * g + 1, 16 + g, 24 + g)
                    for ob in range(4):
                        self.cp(POOL, rw.t[:, ob, 0:3], xbc_prev.t[:, idxs[ob], :], [xbc_prev], [rw])
                    for ob in range(4):
                        pb = self.pbank()
                        for k in range(8):
                            self.mm(pb.t[:, :], wv[:, k, ob * 128:(ob + 1) * 128], uT.t[:, k, :], k == 0, k == 7,
                                    [slot, uT], [pb])
                        self.act(rw.t[:, ob, 3:T + 3], pb.t[:, :], AF.Copy, [pb], [rw])
                    for ob in range(4):
                        self.cp(POOL, xbc_prev.t[:, idxs[ob], :], rw.t[:, ob, T:T + 3], [rw], [xbc_prev])
                    xg = xTg[0]
                    bg = bTg[g % 2]
                    for ob in range(4):
                        ci = idxs[ob]
                        ac = accf[ob % 2]
                        self.ts(DVE, ac.t[:, :], rw.t[:, ob, 0:T], sw(0, ci), ALU.mult, [rw, params], [ac],
                                s2=sbias(ci), op1=ALU.add)
                        for kk in range(1, 4):
                            self.stt(ac.t[:, :], rw.t[:, ob, kk:T + kk], sw(kk, ci), ac.t[:, :], ALU.mult, ALU.add,
                                     [rw, params, ac], [ac])
                        if ob < 2:
                            self.act(xg.t[:, ob, :], ac.t[:, :], AF.Silu, [ac], [xg])
                        elif ob == 2:
                            self.act(bg.t[:, :], ac.t[:, :], AF.Silu, [ac], [bg])
                        else:
                            self.act(ct_all.t[:, g, :], ac.t[:, :], AF.Silu, [ac], [ct_all])
                    xm = xtm[0]
                    bm = btm[0]
                    tbk = self.tbank()
                    for tb in range(4):
                        for ob in range(2):
                            self.tr(tbk.t[:, (tb * 2 + ob) * 128:(tb * 2 + ob + 1) * 128],
                                    xg.t[:, ob, tb * 128:(tb + 1) * 128], ident_b, [xg, cbf], [tbk])
                    self.cp(ACT, xm.t[:, :, :], tbk.t[:, :].rearrange("p (a b) -> p a b", a=4), [tbk], [xm])
                    tbk = self.tbank()
                    for tb in range(4):
                        self.tr(tbk.t[:, tb * 128:(tb + 1) * 128], bg.t[:, tb * 128:(tb + 1) * 128], ident_b,
                                [bg, cbf], [tbk])
                    self.cp(ACT, bm.t[:, :, :], tbk.t[:, 0:512].rearrange("p (a b) -> p a b", a=4), [tbk], [bm])
                    yl_dst = p4 if True else None
                    for tb in range(4):
                        i2 = (g * 4 + tb) % 2
                        hs = slice(4 * g, 4 * g + 4)
                        pseg = self.pbank()
                        for xi in range(2):
                            ar = arhs[xi]
                            self.tt(POOL, ar.t[:, :, :], a3[xi].t[:, tb, hs].unsqueeze(2).to_broadcast([128, 4, 128]),
                                    triu_b.unsqueeze(1).to_broadcast([128, 4, 128]), ALU.mult, [a3[xi], cbf], [ar])
                            self.mm(pseg.t[:, :], lmat_b, ar.t[:, :, :].rearrange("p a b -> p (a b)"), xi == 0, xi == 1,
                                    [cbf, ar], [pseg])
                        eg = eseg_g[i2]
                        self.act(eg.t[:, :, :], pseg.t[:, :].rearrange("p (a b) -> p a b", a=4), AF.Exp, [pseg], [eg])
                        psc = self.pbank()
                        self.mm(psc.t[:, 0:128], bg.t[:, tb * 128:(tb + 1) * 128], ct_all.t[:, g, tb * 128:(tb + 1) * 128],
                                True, True, [bg, ct_all], [psc])
                        ms = msc[i2]
                        self.tt(DVE, ms.t[:, :], psc.t[:, 0:128], triu_f, ALU.mult, [psc, consts], [ms])
                        mt = mT[i2]
                        self.tt(POOL, mt.t[:, :, :], eg.t[:, :, :], ms.t[:, :].unsqueeze(1).to_broadcast([128, 4, 128]),
                                ALU.mult, [eg, ms], [mt])
                        xd = xdt[i2]
                        xe = xdte[i2]
                        self.tt(POOL, xd.t[:, :].rearrange("p (a b) -> p a b", a=4),
                                xm.t[:, tb, :].rearrange("p (a b) -> p a b", a=4),
                                dtt.t[:, tb, hs].unsqueeze(2).to_broadcast([128, 4, 64]), ALU.mult, [xm, dtt], [xd])
                        self.tt(POOL, xe.t[:, :].rearrange("p (a b) -> p a b", a=4),
                                xd.t[:, :].rearrange("p (a b) -> p a b", a=4),
                                toend.t[:, tb, hs].unsqueeze(2).to_broadcast([128, 4, 64]), ALU.mult, [xd, toend], [xe])
                        py = self.pbank()
                        for r in range(4):
                            self.mm(py.t[:, r * 64:(r + 1) * 64], mt.t[:, r, :], xd.t[:, r * 64:(r + 1) * 64],
                                    True, True, [mt, xd], [py])
                        self.mm(py.t[:, 256:512], ct_all.t[:, g, tb * 128:(tb + 1) * 128], state_bf.t[:, g, :],
                                True, True, [ct_all, state_bf], [py])
                        a1 = t1[0]
                        a2 = t2[0]
                        self.tt(DVE, a1.t[:, :].rearrange("p (a b) -> p a b", a=4),
                                py.t[:, 256:512].rearrange("p (a b) -> p a b", a=4),
                                eloc.t[:, tb, hs].unsqueeze(2).to_broadcast([128, 4, 64]), ALU.mult, [py, eloc], [a1])
                        self.tt(POOL, a2.t[:, :].rearrange("p (a b) -> p a b", a=4),
                                xm.t[:, tb, :].rearrange("p (a b) -> p a b", a=4),
                                self.par("dskip", l * 32 + 4 * g, l * 32 + 4 * g + 4).unsqueeze(2).to_broadcast([128, 4, 64]),
                                ALU.mult, [xm, params], [a2])
                        self.tt(POOL, a2.t[:, :], a2.t[:, :], a1.t[:, :], ALU.add, [a1, a2], [a2])
                        ydst = (p4 if tb < 2 else p5)
                        yv = ydst.t[:, :, :].rearrange("p a b -> p (a b)")[:, (tb % 2) * 2048 + g * 256:(tb % 2) * 2048 + (g + 1) * 256]
                        self.tt(DVE, yv, py.t[:, 0:256], a2.t[:, :], ALU.add, [py, a2], [ydst])
                        pst = self.pbank()
                        self.mm(pst.t[:, 0:256], bm.t[:, tb, :], xe.t[:, :], True, True, [bm, xe], [pst])
                        self.tt(DVE, state.t[:, g, :].rearrange("p (a b) -> p a b", a=4),
                                state.t[:, g, :].rearrange("p (a b) -> p a b", a=4),
                                dk.t[:, tb, hs].unsqueeze(2).to_broadcast([128, 4, 64]), ALU.mult, [state, dk], [state])
                        self.tt(DVE, state.t[:, g, :], state.t[:, g, :], pst.t[:, 0:256], ALU.add, [state, pst], [state])
                        self.cp(ACT, state_bf.t[:, g, :], state.t[:, g, :], [state], [state_bf])
                    self.chk(9)
                p4f = p4.t[:, :, :].rearrange("p a b -> p (a b)")
                p5f = p5.t[:, :, :].rearrange("p a b -> p (a b)")
                self.dma(SP, (lambda it: (lambda e: e.dma_start(out=sp_yl[it][:, 0:4096], in_=p4f[:, 0:4096])))(it),
                         [p4], [R_yl[it]])
                self.dma(SP, (lambda it: (lambda e: e.dma_start(out=sp_yl[it][:, 4096:8192], in_=p5f[:, 0:4096])))(it),
                         [p5], [R_yl[it]])
                self.dma(SP, (lambda it: (lambda e: e.dma_start(out=sp_ct[it].rearrange("p (a b) -> p a b", a=8),
                                                                 in_=ct_all.t[:, :, :])))(it), [ct_all], [R_ct[it]])
                self.chk(10)

            self.chk(11)

            for it in range(NT):
                ts_ = it * T
                ctl, gcp, gss = gA, gA, gB
                self.dma(SP, (lambda it: (lambda e: e.dma_start(out=xT.t[:, :, :], in_=xres[:, :, it * T:(it + 1) * T])))(it),
                         [R_x[it]], [xT])
                self.dma(SP, (lambda it: (lambda e: e.dma_start(out=p4.t[:, :, :].rearrange("p a b -> p (a b)")[:, 0:4096],
                                                                 in_=sp_yl[it][:, 0:4096])))(it), [R_yl[it]], [p4])
                self.dma(SP, (lambda it: (lambda e: e.dma_start(out=p5.t[:, :, :].rearrange("p a b -> p (a b)")[:, 0:4096],
                                                                 in_=sp_yl[it][:, 4096:8192])))(it), [R_yl[it]], [p5])
                f32af = f32a.t[:, :, :].rearrange("p a b -> p (a b)")
                sqv = gB.t[:, :, :].rearrange("p a b -> p (a b)")[:, 0:2048]
                for tb in range(4):
                    zt = raw[tb % 2]
                    ztf = zt.t[:, :, :].rearrange("p a b -> p (a b)")[:, 0:2048]
                    self.dma(SP, (lambda it, tb, ztf: (lambda e: e.dma_start(out=ztf,
                                                                              in_=sp_sz[it][:, tb * 2048:(tb + 1) * 2048])))(it, tb, ztf),
                             [R_sz[it]], [zt])
                    ysrc = (p4 if tb < 2 else p5).t[:, :, :].rearrange("p a b -> p (a b)")[:, (tb % 2) * 2048:(tb % 2 + 1) * 2048]
                    ysrc_t = p4 if tb < 2 else p5
                    self.cp(POOL, f32af, ysrc, [ysrc_t], [f32a])
                    self.tt(POOL, f32af, f32af, ztf, ALU.mult, [f32a, zt], [f32a])
                    self.tt(DVE, sqv, f32af, f32af, ALU.mult, [f32a], [gB])
                    self.op(DVE, lambda e: e.tensor_reduce(out=ssq.t[:, :], in_=sqv.rearrange("p (a b) -> p a b", a=8),
                                                           axis=AX.X, op=ALU.add), [gB], [ssq])
                    self.ts(DVE, rs8.t[:, :], ssq.t[:, :], 1.0 / 256.0, ALU.mult, [ssq, params], [rs8],
                            s2=self.par("eps"), op1=ALU.add)
                    self.act(rs8.t[:, :], rs8.t[:, :], AF.Ln, [rs8], [rs8])
                    self.act(rs8.t[:, :], rs8.t[:, :], AF.Exp, [rs8], [rs8], scale=-0.5)
                    self.tt(DVE, ztf.rearrange("p (a b) -> p a b", a=8),
                            f32af.rearrange("p (a b) -> p a b", a=8),
                            rs8.t[:, :].unsqueeze(2).to_broadcast([128, 8, 256]), ALU.mult, [f32a, rs8], [zt])
                    for hf in range(2):
                        tbk = self.tbank()
                        for c8 in range(8):
                            cbk = hf * 8 + c8
                            self.tr(tbk.t[:, c8 * 128:(c8 + 1) * 128], ztf[:, cbk * 128:(cbk + 1) * 128], ident_b,
                                    [zt, cbf], [tbk])
                        ydst_t = uT if hf == 0 else p3
                        self.tt(DVE, ydst_t.t[:, :, tb * 128:(tb + 1) * 128], tbk.t[:, :].rearrange("p (a b) -> p a b", a=8),
                                self.par("nw", l * 16 + hf * 8, l * 16 + hf * 8 + 8).unsqueeze(2).to_broadcast([128, 8, 128]),
                                ALU.mult, [tbk, params], [ydst_t])
                self.chk(13)
                self.dma(SP, (lambda it: (lambda e: e.dma_start(out=gcp.t[:, :, :],
                                                                 in_=sp_gcpc[it].rearrange("p (a b) -> p a b", a=8))))(it),
                         [R_gcpc[it]], [gcp])
                self.dma(SP, (lambda it: (lambda e: e.dma_start(out=gss.t[:, :, :],
                                                                 in_=sp_gssm[it].rearrange("p (a b) -> p a b", a=8))))(it),
                         [R_gssm[it]], [gss])
                merged = p4
                for b in range(4):
                    slot = self.wnext()
                    wv = self.wview(slot, 16, 256)
                    for c2 in range(2):
                        ob = b * 2 + c2
                        pb = self.pbank()
                        for k in range(16):
                            rhs = uT.t[:, k, :] if k < 8 else p3.t[:, k - 8, :]
                            self.mm(pb.t[:, :], wv[:, k, c2 * 128:(c2 + 1) * 128], rhs, k == 0, k == 15,
                                    [slot, uT, p3], [pb])
                        ac = accf[ob % 2]
                        self.tt(DVE, ac.t[:, :], pb.t[:, :], gss.t[:, ob, :], ALU.mult, [pb, gss], [ac])
                        self.tt(POOL, merged.t[:, ob, 0:T], ac.t[:, :], gcp.t[:, ob, :], ALU.add, [ac, gcp], [merged])
                for b in range(2):
                    slot = self.wnext()
                    wv = self.wview(slot, 8, 512)
                    for c4 in range(4):
                        ob = b * 4 + c4
                        pb = self.pbank()
                        for k in range(8):
                            self.mm(pb.t[:, :], wv[:, k, c4 * 128:(c4 + 1) * 128], merged.t[:, k, 0:T], k == 0, k == 7,
                                    [slot, merged], [pb])
                        self.stt(xT.t[:, ob, :], pb.t[:, :], lay.t[:, 2, ob:ob + 1], xT.t[:, ob, :],
                                 ALU.mult, ALU.add, [pb, lay, xT], [xT])
                self.chk(14)
                norm_to_u(ts_, 3, 4)
                hid = p5
                for fg in range(4):
                    for b in range(2):
                        slot = self.wnext()
                        wv = self.wview(slot, 8, 512)
                        for c4 in range(4):
                            fb = b * 4 + c4
                            pb = self.pbank()
                            for k in range(8):
                                self.mm(pb.t[:, :], wv[:, k, c4 * 128:(c4 + 1) * 128], uT.t[:, k, :], k == 0, k == 7,
                                        [slot, uT], [pb])
                            rl = bTg[fb % 2]
                            self.act(rl.t[:, :], pb.t[:, :], AF.Relu, [pb], [rl])
                            self.tt(POOL, hid.t[:, fb, :], rl.t[:, :], rl.t[:, :], ALU.mult, [rl], [hid])
                    for b in range(2):
                        slot = self.wnext()
                        wv = self.wview(slot, 8, 512)
                        for c4 in range(4):
                            ob = b * 4 + c4
                            pb = self.pbank()
                            for k in range(8):
                                self.mm(pb.t[:, :], wv[:, k, c4 * 128:(c4 + 1) * 128], hid.t[:, k, :], k == 0, k == 7,
                                        [slot, hid], [pb])
                            self.stt(xT.t[:, ob, :], pb.t[:, :], lay.t[:, 5, ob:ob + 1],
                                     xT.t[:, ob, :], ALU.mult, ALU.add, [pb, lay, xT], [xT])
                self.dma(SP, (lambda it: (lambda e: e.dma_start(out=xres[:, :, it * T:(it + 1) * T], in_=xT.t[:, :, :])))(it),
                         [xT], [R_x[it]])

    def finish(self, env):
        (nc, xT, params, cbf, lay, f32a, p3, rstd, accf, ident_f, onesd_b, out, PE, ACT, DVE, POOL, SP, xres, R_x) = [env[k] for k in (
            "nc xT params cbf lay f32a p3 rstd accf ident_f onesd_b out PE ACT DVE POOL SP xres R_x").split()]
        self.cp(DVE, lay.t[:, 0, :], self.par("fn"), [params], [lay])
        self.op(POOL, lambda e: e.memset(lay.t[:, 1, :], 0.0), [], [lay])
        for it in range(NT):
            ts_ = it * T
            self.dma(SP, (lambda it: (lambda e: e.dma_start(out=xT.t[:, :, :], in_=xres[:, :, it * T:(it + 1) * T])))(it),
                     [R_x[it]], [xT])
            for h in range(2):
                tsl = slice(h * 256, (h + 1) * 256)
                self.act(p3.t[:, :, h * 256:(h + 1) * 256], xT.t[:, :, tsl], AF.Square, [xT], [p3])
            pb = self.pbank()
            for k in range(8):
                self.mm(pb.t[:, :], onesd_b, p3.t[:, k, :], k == 0, k == 7, [cbf, p3], [pb])
            self.act(rstd.t[:, :], pb.t[:, :], AF.Ln, [pb, params], [rstd], bias=self.par("eps"))
            self.act(rstd.t[:, :], rstd.t[:, :], AF.Exp, [rstd], [rstd], scale=-0.5)
            for h in range(2):
                tsl = slice(h * 256, (h + 1) * 256)
                self.tt(DVE, f32a.t[:, :, :], xT.t[:, :, tsl],
                        rstd.t[:, h * 256:(h + 1) * 256].unsqueeze(1).to_broadcast([128, 8, 256]),
                        ALU.mult, [xT, rstd], [f32a])
                for j in range(8):
                    self.act(f32a.t[:, j, :], f32a.t[:, j, :], AF.Identity, [f32a, lay], [f32a], scale=lay.t[:, 0, j:j + 1])
                for t2_ in range(2):
                    tb = it * 4 + h * 2 + t2_
                    for hh in range(2):
                        st = accf[hh]
                        pb2 = self.pbank()
                        for jj in range(4):
                            j = hh * 4 + jj
                            self.tr(pb2.t[:, jj * 128:(jj + 1) * 128], f32a.t[:, j, t2_ * 128:(t2_ + 1) * 128], ident_f,
                                    [f32a], [pb2])
                        self.cp(ACT if hh else DVE, st.t[:, :], pb2.t[:, :], [pb2], [st])
                        self.dma(SP, (lambda tb, hh, st: (lambda e: e.dma_start(
                            out=out[tb * 128:(tb + 1) * 128, hh * 512:(hh + 1) * 512], in_=st.t[:, :])))(tb, hh, st),
                            [st], [self.dbg_res], dsem=self.out_sem)
        SP.prog.append(("wait", self.out_sem, self.out_sem.count))

        def replay(Q):
            def body(e):
                for item in Q.prog:
                    if item[0] == "wait":
                        e.wait_ge(item[1].h, item[2])
                    elif item[0] == "op":
                        inst = item[1](e)
                        if item[2]:
                            inst.then_inc(Q.sem.h, 1)
                    elif item[0] == "cc":
                        item[1](e).then_inc(item[2].h)
                    else:
                        item[1](e).then_inc(item[2].h, 16)
            return body

        with nc.Block() as block:
            block.tensor(replay(PE))
            block.scalar(replay(ACT))
            block.vector(replay(DVE))
            block.gpsimd(replay(POOL))
            block.sync(replay(SP))
        self.stack.close()
        return nc


def _pm(a, n):
    a = np.asarray(a, np.float32)
    lead = a.shape[:-1]
    r = a.reshape(lead + (n, 128))
    return np.moveaxis(r, -1, 0)


def _host_inputs(inputs, nl=DEPTH):
    consts = np.zeros((128, 512), np.float32)
    consts[:, 0:128] = np.eye(128, dtype=np.float32)
    s = np.arange(128)
    consts[:, 128:256] = (s[:, None] <= s[None, :]).astype(np.float32)
    consts[:, 256:384] = (s[:, None] > s[None, :]).astype(np.float32)
    consts[:, 384:512] = 1.0
    per_core = []
    for k in range(NCORES):
        b, q = k, 0
        P = np.zeros((128, NPAR), np.float32)

        def put(name, arr):
            o, w = _cols[name]
            P[:, o:o + w] = np.asarray(arr, np.float32).reshape(128, w)

        put("c", _pm(inputs["c"][b], 8))
        put("b_ada", _pm(inputs["b_ada"], 48))
        put("ln1", _pm(inputs["ln1"], 8))
        put("ln2", _pm(inputs["ln2"], 8))
        put("fn", _pm(inputs["final_norm"], 8))
        put("convw", _pm(inputs["conv_w"], 8))
        put("sconvw", _pm(inputs["ssm_conv_w"], 32))
        put("sconvb", _pm(inputs["ssm_conv_b"], 32))
        put("dtb", np.broadcast_to(np.asarray(inputs["dt_bias"], np.float32).reshape(1, 128), (128, 128)))
        put("alog", np.broadcast_to(np.asarray(inputs["a_log"], np.float32).reshape(1, 128), (128, 128)))
        put("dskip", np.broadcast_to(np.asarray(inputs["d_skip"], np.float32).reshape(1, 128), (128, 128)))
        put("nw", _pm(inputs["ssm_norm_w"], 16))
        sel = np.zeros(8, np.float32)
        mk = np.zeros(8, np.float32)
        bt = np.zeros((8, 8), np.float32)
        put("sel", np.broadcast_to(sel.reshape(1, 8), (128, 8)))
        put("mk", np.broadcast_to(mk.reshape(1, 8), (128, 8)))
        put("bt", np.broadcast_to(bt.reshape(1, 64), (128, 64)))
        put("one", np.ones((128, 1), np.float32))
        put("eps", np.full((128, 1), EPS, np.float32))
        m = {
            "x": np.ascontiguousarray(np.asarray(inputs["x"], np.float32)[b, q * NTOK:(q + 1) * NTOK, :]),
            "params": P,
            "consts": consts,
        }
        for wn in ("w_ada", "w_in", "w_conv_out", "w_ssm_out", "w_o", "w_up", "w_down"):
            m[wn] = np.ascontiguousarray(np.asarray(inputs[wn], np.float32)[:max(nl, 1)])
        per_core.append(m)
    return per_core


_CACHE = {}


def _run(inputs, nl=DEPTH, debug=(), lim=10 ** 9):
    key = (nl, tuple(debug), lim)
    if key not in _CACHE:
        bld = Builder(nl, debug, lim)
        _CACHE[key] = (bld.build(), bld)
    nc, bld = _CACHE[key]
    in_maps = _host_inputs(inputs, nl)
    res = run_bass_kernel_spmd(nc, in_maps, core_ids=list(range(NCORES)))
    out = np.empty((2, SEQ, D), np.float32)
    for k in range(NCORES):
        out[k, :, :] = res.results[k]["out"]
    return out, res


def kernel(**inputs):
    out, _ = _run(inputs, DEPTH)
    return out
```
